# Optimizing a Trainium2 kernel written in Bass

```python
import jax, jax.numpy as jnp
from jax import lax
import numpy as np

D_MODEL = 1024
BATCH = 8
SEQ = 2048
DEPTH = 1
DEC_BATCH = 128
DEC_SEQ = 1
PAST_LEN = 16384
PAGE_SIZE = 128

D_MIX = D_MODEL
D_CONV = D_MIX // 2
D_LRU = D_MIX - D_CONV
N_LRU_HEADS = 8
LRU_HEAD_DIM = D_LRU // N_LRU_HEADS
CONV_A_WIDTH = 3
CONV_B_WIDTH = 4
LRU_C = 8.0
IN_COLS = 3 * D_CONV + 2 * D_LRU
FFN_DIM = ((8 * D_MODEL // 3 + 127) // 128) * 128
N_MEM = 256
N_XHEADS = 4
XHEAD_DIM = D_MODEL // N_XHEADS
RMS_EPS = 1e-6

kernel_name = "hymba_conv_rglru_macaron_step"


def rms_norm(x, g):
    xf = x.astype(jnp.float32)
    y = xf * lax.rsqrt(jnp.mean(xf * xf, axis=-1, keepdims=True) + RMS_EPS)
    return (y * g.astype(jnp.float32)).astype(x.dtype)


def swiglu(h, w_gate, w_up, w_down):
    return (jax.nn.silu(h @ w_gate) * (h @ w_up)) @ w_down


def causal_dwconv(buf, u, w):
    width = w.shape[0]
    t = u.shape[1]
    up = jnp.concatenate([buf.astype(u.dtype), u], axis=1)
    y = up[:, 0:t] * w[0]
    for k in range(1, width):
        y = y + up[:, k:k + t] * w[k]
    return y, up[:, t:]


def rg_lru(u, h0, w_a, b_a, w_x, b_x, lam, reset_first):
    n, t, _ = u.shape
    uh = u.reshape(n, t, N_LRU_HEADS, LRU_HEAD_DIM)
    gate_a = jnp.einsum('nthi,hij->nthj', uh, w_a).reshape(n, t, D_LRU) + b_a
    gate_x = jnp.einsum('nthi,hij->nthj', uh, w_x).reshape(n, t, D_LRU) + b_x
    r = jax.nn.sigmoid(gate_a.astype(jnp.float32))
    i = jax.nn.sigmoid(gate_x.astype(jnp.float32))
    log_a = -LRU_C * r * jax.nn.softplus(-lam.astype(jnp.float32))
    a = jnp.exp(log_a)
    mult = jnp.sqrt(-jnp.expm1(2.0 * log_a))
    if reset_first:
        mult = mult.at[:, 0].set(1.0)
    b = mult * i * u.astype(jnp.float32)

    def combine(left, right):
        a1, b1 = left
        a2, b2 = right
        return a1 * a2, a2 * b1 + b2

    a_cum, b_cum = lax.associative_scan(combine, (a, b), axis=1)
    h = a_cum * h0.astype(jnp.float32)[:, None, :] + b_cum
    return h.astype(u.dtype), h[:, -1].astype(h0.dtype)


def memory_kv(mem, g_mem, w_k, w_v):
    m = rms_norm(mem, g_mem)
    n, s, _ = mem.shape
    k = (m @ w_k).reshape(n, s, N_XHEADS, XHEAD_DIM)
    v = (m @ w_v).reshape(n, s, N_XHEADS, XHEAD_DIM)
    return k, v


def cross_attn(h, mem_k, mem_v, w_q, w_o):
    n, t, _ = h.shape
    q = (h @ w_q).reshape(n, t, N_XHEADS, XHEAD_DIM)
    s = jnp.einsum('nthd,nmhd->nhtm', q, mem_k.astype(q.dtype)).astype(jnp.float32) * (XHEAD_DIM ** -0.5)
    p = jax.nn.softmax(s, axis=-1).astype(h.dtype)
    o = jnp.einsum('nhtm,nmhd->nthd', p, mem_v.astype(h.dtype)).reshape(n, t, D_MODEL)
    return o @ w_o


def token_mixing(h, conv_a_buf, conv_b_buf, lru_h0, reset_first, p):
    z = h @ p['w_in']
    gb, gc, xa, xb, gg = jnp.split(z, [D_CONV, 2 * D_CONV, 3 * D_CONV, 3 * D_CONV + D_LRU], axis=-1)
    ca, new_a = causal_dwconv(conv_a_buf, gc * xa, p['conv_a_w'])
    ya = gb * ca
    cb, new_b = causal_dwconv(conv_b_buf, xb, p['conv_b_w'])
    cb = cb + p['conv_b_b']
    hb, h_last = rg_lru(cb, lru_h0, p['lru_wa'], p['lru_ba'], p['lru_wx'], p['lru_bx'], p['lru_lam'], reset_first)
    yb = jax.nn.gelu(gg) * hb
    y = jnp.concatenate([ya, yb], axis=-1) @ p['w_out']
    return y, new_a, new_b, h_last


def decoder_layer(x, mem_k, mem_v, conv_a_buf, conv_b_buf, lru_h0, reset_first, p):
    x = x + 0.5 * rms_norm(swiglu(rms_norm(x, p['g_ffn1_pre']), p['ffn1_wg'], p['ffn1_wu'], p['ffn1_wd']), p['g_ffn1_post'])
    mix, new_a, new_b, h_last = token_mixing(rms_norm(x, p['g_mix_pre']), conv_a_buf, conv_b_buf, lru_h0, reset_first, p)
    x = x + rms_norm(mix, p['g_mix_post'])
    x = x + rms_norm(cross_attn(rms_norm(x, p['g_xattn_pre']), mem_k, mem_v, p['xattn_wq'], p['xattn_wo']), p['g_xattn_post'])
    x = x + 0.5 * rms_norm(swiglu(rms_norm(x, p['g_ffn2_pre']), p['ffn2_wg'], p['ffn2_wu'], p['ffn2_wd']), p['g_ffn2_post'])
    return x, new_a, new_b, h_last


def setup_inputs(seed: int = 0) -> dict:
    key = jax.random.key(seed)
    keys = iter(jax.random.split(key, 48))
    f32 = jnp.float32

    def nrm(shape, scale=1.0):
        return jax.random.normal(next(keys), shape, f32) * scale

    def gain():
        return 1.0 + nrm((DEPTH, D_MODEL), 0.05)

    u = jax.random.uniform(next(keys), (DEPTH, D_LRU), f32, minval=0.9, maxval=0.999)
    a0 = u ** (1.0 / LRU_C)
    lru_lam = jnp.log(a0) - jnp.log1p(-a0)
    return {
        "x_prompt": nrm((BATCH, SEQ, D_MODEL)),
        "x_sample": nrm((DEC_BATCH, DEC_SEQ, D_MODEL)),
        "mem_prompt": nrm((BATCH, N_MEM, D_MODEL)),
        "cache_mem_k": nrm((DEPTH, DEC_BATCH, N_MEM, N_XHEADS, XHEAD_DIM)),
        "cache_mem_v": nrm((DEPTH, DEC_BATCH, N_MEM, N_XHEADS, XHEAD_DIM)),
        "state_conv_a": nrm((DEPTH, DEC_BATCH, CONV_A_WIDTH - 1, D_CONV)),
        "state_conv_b": nrm((DEPTH, DEC_BATCH, CONV_B_WIDTH - 1, D_LRU)),
        "state_lru": nrm((DEPTH, DEC_BATCH, D_LRU), 0.5),
        "g_ffn1_pre": gain(), "g_ffn1_post": gain(),
        "ffn1_wg": nrm((DEPTH, D_MODEL, FFN_DIM), D_MODEL ** -0.5),
        "ffn1_wu": nrm((DEPTH, D_MODEL, FFN_DIM), D_MODEL ** -0.5),
        "ffn1_wd": nrm((DEPTH, FFN_DIM, D_MODEL), FFN_DIM ** -0.5),
        "g_mix_pre": gain(), "g_mix_post": gain(),
        "w_in": nrm((DEPTH, D_MODEL, IN_COLS), D_MODEL ** -0.5),
        "conv_a_w": nrm((DEPTH, CONV_A_WIDTH, D_CONV), CONV_A_WIDTH ** -0.5),
        "conv_b_w": nrm((DEPTH, CONV_B_WIDTH, D_LRU), CONV_B_WIDTH ** -0.5),
        "conv_b_b": nrm((DEPTH, D_LRU), 0.01),
        "lru_wa": nrm((DEPTH, N_LRU_HEADS, LRU_HEAD_DIM, LRU_HEAD_DIM), LRU_HEAD_DIM ** -0.5),
        "lru_ba": nrm((DEPTH, D_LRU), 0.01),
        "lru_wx": nrm((DEPTH, N_LRU_HEADS, LRU_HEAD_DIM, LRU_HEAD_DIM), LRU_HEAD_DIM ** -0.5),
        "lru_bx": nrm((DEPTH, D_LRU), 0.01),
        "lru_lam": lru_lam,
        "w_out": nrm((DEPTH, D_MIX, D_MODEL), D_MIX ** -0.5),
        "g_xattn_pre": gain(), "g_xattn_post": gain(), "g_mem": gain(),
        "xattn_wq": nrm((DEPTH, D_MODEL, D_MODEL), D_MODEL ** -0.5),
        "xattn_wk": nrm((DEPTH, D_MODEL, D_MODEL), D_MODEL ** -0.5),
        "xattn_wv": nrm((DEPTH, D_MODEL, D_MODEL), D_MODEL ** -0.5),
        "xattn_wo": nrm((DEPTH, D_MODEL, D_MODEL), D_MODEL ** -0.5),
        "g_ffn2_pre": gain(), "g_ffn2_post": gain(),
        "ffn2_wg": nrm((DEPTH, D_MODEL, FFN_DIM), D_MODEL ** -0.5),
        "ffn2_wu": nrm((DEPTH, D_MODEL, FFN_DIM), D_MODEL ** -0.5),
        "ffn2_wd": nrm((DEPTH, FFN_DIM, D_MODEL), FFN_DIM ** -0.5),
    }


def reference(x_prompt, x_sample, mem_prompt, cache_mem_k, cache_mem_v, state_conv_a, state_conv_b, state_lru,
              g_ffn1_pre, g_ffn1_post, ffn1_wg, ffn1_wu, ffn1_wd,
              g_mix_pre, g_mix_post, w_in, conv_a_w, conv_b_w, conv_b_b,
              lru_wa, lru_ba, lru_wx, lru_bx, lru_lam, w_out,
              g_xattn_pre, g_xattn_post, g_mem, xattn_wq, xattn_wk, xattn_wv, xattn_wo,
              g_ffn2_pre, g_ffn2_post, ffn2_wg, ffn2_wu, ffn2_wd):
    yp, ys = x_prompt, x_sample
    nb = x_prompt.shape[0]
    mk_p_l, mv_p_l, ca_p_l, cb_p_l, h_p_l = [], [], [], [], []
    ca_s_l, cb_s_l, h_s_l = [], [], []
    for l in range(DEPTH):
        p = {
            'g_ffn1_pre': g_ffn1_pre[l], 'g_ffn1_post': g_ffn1_post[l],
            'ffn1_wg': ffn1_wg[l], 'ffn1_wu': ffn1_wu[l], 'ffn1_wd': ffn1_wd[l],
            'g_mix_pre': g_mix_pre[l], 'g_mix_post': g_mix_post[l], 'w_in': w_in[l],
            'conv_a_w': conv_a_w[l], 'conv_b_w': conv_b_w[l], 'conv_b_b': conv_b_b[l],
            'lru_wa': lru_wa[l], 'lru_ba': lru_ba[l], 'lru_wx': lru_wx[l], 'lru_bx': lru_bx[l],
            'lru_lam': lru_lam[l], 'w_out': w_out[l],
            'g_xattn_pre': g_xattn_pre[l], 'g_xattn_post': g_xattn_post[l],
            'xattn_wq': xattn_wq[l], 'xattn_wo': xattn_wo[l],
            'g_ffn2_pre': g_ffn2_pre[l], 'g_ffn2_post': g_ffn2_post[l],
            'ffn2_wg': ffn2_wg[l], 'ffn2_wu': ffn2_wu[l], 'ffn2_wd': ffn2_wd[l],
        }
        mk_p, mv_p = memory_kv(mem_prompt, g_mem[l], xattn_wk[l], xattn_wv[l])
        za = jnp.zeros((nb, CONV_A_WIDTH - 1, D_CONV), yp.dtype)
        zb = jnp.zeros((nb, CONV_B_WIDTH - 1, D_LRU), yp.dtype)
        zh = jnp.zeros((nb, D_LRU), yp.dtype)
        yp, ca_p, cb_p, h_p = decoder_layer(yp, mk_p, mv_p, za, zb, zh, True, p)
        ys, ca_s, cb_s, h_s = decoder_layer(ys, cache_mem_k[l], cache_mem_v[l], state_conv_a[l], state_conv_b[l],
                                            state_lru[l], False, p)
        mk_p_l.append(mk_p); mv_p_l.append(mv_p); ca_p_l.append(ca_p); cb_p_l.append(cb_p); h_p_l.append(h_p)
        ca_s_l.append(ca_s); cb_s_l.append(cb_s); h_s_l.append(h_s)
    return (yp, ys, jnp.stack(mk_p_l), jnp.stack(mv_p_l), jnp.stack(ca_p_l), jnp.stack(cb_p_l), jnp.stack(h_p_l),
            jnp.stack(ca_s_l), jnp.stack(cb_s_l), jnp.stack(h_s_l))
```

```python
import numpy as np
import concourse.bass as bass
import concourse.mybir as mybir
from concourse.bass_utils import run_bass_kernel_spmd
from contextlib import ExitStack

F32 = mybir.dt.float32
BF16 = mybir.dt.bfloat16
AF = mybir.ActivationFunctionType
ALU = mybir.AluOpType
AX = mybir.AxisListType

D = 1024
FF = 2816
T = 2048
PT = 512
NPASS = T // PT
NSMP = 16
NM = 256
W_ALL = PT + NSMP
NSLOT = 6
SLOT_ELEMS = 22 * 128
LOOKAHEAD = NSLOT - 1
EPS = 1e-6
NBANK = 5
K_PRE = 20
K_POST = 28
C0 = 0.7978845608028654
C1 = 0.044715
DEBUG = {}
PHASE = ['init']
CURP = ['-']
FLAT = True


class NT:
    def __init__(self, i, c0, w, sample=False):
        self.i = i
        self.c0 = c0
        self.w = w
        self.sample = sample
        self.cols = slice(c0, c0 + w)


class Sched:
    ENG = ('pe', 'act', 'dve', 'pool', 'sp')

    def __init__(self):
        self.streams = {e: [] for e in self.ENG}
        self.count = {}
        self.state = {}
        self.waited = {e: {} for e in self.ENG}

    def _deps(self, reads, writes):
        deps = {}

        def add(ev):
            if ev is None:
                return
            s, v = ev
            if deps.get(s, 0) < v:
                deps[s] = v
        for k in reads:
            st = self.state.get(k)
            if st:
                add(st[0])
        for k in writes:
            st = self.state.get(k)
            if st:
                add(st[0])
                for s, v in st[1].items():
                    add((s, v))
        return deps

    def op(self, eng, fn, reads=(), writes=(), dma_sem=None):
        psr = [k for k in reads if isinstance(k, tuple) and k[0] == 'ps']
        if psr:
            reads = [k for k in reads if not (isinstance(k, tuple) and k[0] == 'ps')]
            writes = list(writes) + psr
        deps = self._deps(reads, writes)
        waits = []
        for s, v in deps.items():
            if s == 'pe' and eng == 'pe':
                continue
            if self.waited[eng].get(s, 0) >= v:
                continue
            self.waited[eng][s] = v
            waits.append((s, v))
        if dma_sem is not None:
            self.count[dma_sem] = self.count.get(dma_sem, 0) + 16
            ev = (dma_sem, self.count[dma_sem])
            inc = (dma_sem, 16)
        else:
            self.count[eng] = self.count.get(eng, 0) + 1
            ev = (eng, self.count[eng])
            inc = (eng, 1)
        for k in reads:
            st = self.state.setdefault(k, [None, {}])
            if st[1].get(ev[0], 0) < ev[1]:
                st[1][ev[0]] = ev[1]
        for k in writes:
            self.state[k] = [ev, {}]
        self.streams[eng].append((waits, fn, inc, PHASE[0]))
        return ev


def build(stop=None):
    nc = bass.Bass("TRN2", target_bir_lowering=False)
    S = Sched()
    op = S.op

    def din(name, shape):
        return nc.dram_tensor(name, list(shape), F32, kind="ExternalInput").ap()

    def dout(name, shape):
        return nc.dram_tensor(name, list(shape), F32, kind="ExternalOutput").ap()

    x_d = din("x", [T, D])
    xs_d = din("xs", [NSMP, D])
    mem_d = din("mem", [NM, D])
    ck_d = din("ck", [NSMP, NM, D])
    cv_d = din("cv", [NSMP, NM, D])
    sca_d = din("sca", [NSMP, 2 * 512])
    scb_d = din("scb", [NSMP, 3 * 512])
    slru_d = din("slru", [NSMP, 512])
    vec1024 = ["g_ffn1_pre", "g_ffn1_post", "g_mix_pre", "g_mix_post", "g_xattn_pre", "g_xattn_post",
               "g_mem", "g_ffn2_pre", "g_ffn2_post"]
    vec_d = {n: din(n, [1, D]) for n in vec1024}
    caw_d = din("conv_a_w", [3, 512])
    cbw_d = din("conv_b_w", [4, 512])
    v512 = ["conv_b_b", "lru_ba", "lru_bx", "lru_lam"]
    v512_d = {n: din(n, [1, 512]) for n in v512}
    lwa_d = din("lru_wa", [8, 64, 64])
    lwx_d = din("lru_wx", [8, 64, 64])
    Wd = {}
    for n, shp in [("ffn1_wg", (D, FF)), ("ffn1_wu", (D, FF)), ("ffn1_wd", (FF, D)), ("w_in", (D, 2560)),
                   ("w_out", (D, D)), ("xattn_wq", (D, D)), ("xattn_wk", (D, D)), ("xattn_wv", (D, D)),
                   ("xattn_wo", (D, D)), ("ffn2_wg", (D, FF)), ("ffn2_wu", (D, FF)), ("ffn2_wd", (FF, D))]:
        Wd[n] = din(n, shp)

    y_d = dout("y", [T, D])
    ys_d = dout("ys", [NSMP, D])
    mk_d = dout("mk", [NM, D])
    mv_d = dout("mv", [NM, D])
    cap_d = dout("cap", [2, 512])
    cbp_d = dout("cbp", [3, 512])
    hp_d = dout("hp", [1, 512])
    cas_d = dout("cas", [NSMP, 2 * 512])
    cbs_d = dout("cbs", [NSMP, 3 * 512])
    hs_d = dout("hs", [NSMP, 512])

    es = ExitStack()

    def sb(name, shape, dt=F32):
        if not DEBUG.get('flat', FLAT):
            return es.enter_context(nc.sbuf_tensor(name, list(shape), dt))
        esz = 2 if dt == BF16 else 4
        n = 1
        for d_ in shape[1:]:
            n *= d_
        nbytes = n * esz
        assert nbytes % 4 == 0
        t = es.enter_context(nc.sbuf_tensor(name, [shape[0], nbytes // 4], F32))
        ap = t[:, :]
        if dt != F32:
            ap = ap.bitcast(dt)
        if len(shape) == 3:
            ap = ap.rearrange("p (a b) -> p a b", a=shape[1])
        elif len(shape) == 4:
            ap = ap.rearrange("p (a b c) -> p a b c", a=shape[1], b=shape[2])
        return ap

    ident = sb("ident", [128, 128])
    ones32 = sb("ones32", [128, 128])
    ones_bf = sb("ones_bf", [128, 128], BF16)
    epsb = sb("epsb", [128, 8])
    wrm = sb("wrm", [128, 512], BF16)
    xpre = sb("xpre", [128, 2, 1024])
    xT = sb("xT", [128, 8, W_ALL])
    hT = sb("hT", [128, 8, W_ALL], BF16)
    big = sb("big", [128, 22, W_ALL], BF16)
    ybuf = sb("ybuf", [128, 8, 1, 516])
    ybs = sb("ybs", [128, 8, NSMP])
    sq = sb("sq", [128, 4, 512], BF16)
    ring = sb("wslab", [128, NSLOT, SLOT_ELEMS], BF16)
    xin = sb("xin", [128, 2, 1024])
    NV = len(vec1024)
    cst = sb("cst", [128, 8, 32])
    cst5 = sb("cst5", [128, 4, 32])
    NR = 3
    rs = sb("rs", [128, NR, 512])
    NTMP = 8
    tmp = sb("tmp", [128, NTMP, 512])
    cb16 = sb("cb16", [128, 2, 512], BF16)
    bd = sb("bd", [128, 8, 128], BF16)
    carryA = sb("carryA", [128, 4, 2])
    carryB = sb("carryB", [128, 4, 3])
    hcar = sb("hcar", [128, 4, 1])
    KT = sb("KT", [128, 8, NM], BF16)
    Vn = sb("Vn", [128, 2, D], BF16)
    pT = sb("pT", [128, 2, 2, 512], BF16)
    rden = sb("rden", [128, 1, 512])
    Rs = sb("Rs", [128, 24, NSMP])
    SA = sb("SA", [128, 4, 2, NSMP])
    SB_ = sb("SB_", [128, 4, 3, NSMP])
    SH = sb("SH", [128, 4, NSMP])
    qs32 = sb("qs32", [128, 8, NSMP])
    qtok = sb("qtok", [NSMP, 1024], BF16)
    sel = sb("sel", [NSMP, NSMP, 128], BF16)
    Ks = sb("Ks", [128, 2, 2, D], BF16)
    Vs = sb("Vs", [128, 2, 2, D], BF16)
    sc = sb("sc", [128, 8])
    e16 = sb("e16", [128, 2, 8], BF16)
    rdens = sb("rdens", [128, NSMP, 4])
    c_lru = sb("c_lru", [128, 4, 4])
    ps = es.enter_context(nc.psum_tensor("ps", [128, 8, 512], F32))

    vstage = xin[0:32, 0, :]
    v5stage = xin[0:32, 1, 0:512]
    prod = tmp[:, 0:2, :].rearrange("p a b -> p (a b)")
    PRODK = [('tmp', 0), ('tmp', 1)]

    def kt32(c):
        return ybuf[:, c, 0, 0:NM], ('ybuf', c, 0)

    def memT(c):
        return hT[:, c, 0:NM], ('hT', c, 0)
    VI = {n: i for i, n in enumerate(vec1024)}
    GPOST = {"g_ffn1_post": 16, "g_ffn2_post": 17, "g_mix_post": VI["g_mix_post"], "g_xattn_post": VI["g_xattn_post"]}
    V5 = {"caw0": 0, "caw1": 1, "caw2": 2, "cbw0": 3, "cbw1": 4, "cbw2": 5, "cbw3": 6,
          "conv_b_b": 7, "lru_ba": 8, "lru_bx": 9, "lru_lam": 10}

    rot = {}

    def nxt(name, n):
        v = rot.get(name, 0)
        rot[name] = (v + 1) % n
        return v

    def bank():
        return nxt('bank', NBANK)

    dma_sems = set()

    def dma(q, out, in_, sem, reads=(), writes=()):
        dma_sems.add(sem)
        return op(q, lambda e, o=out, i=in_: e.dma_start(out=o, in_=i), reads=reads, writes=writes, dma_sem=sem)

    def act(out, in_, func, reads, writes, **kw):
        return op('act', lambda e: e.activation(out=out, in_=in_, func=func, **kw), reads=reads, writes=writes)

    def tt(eng, out, in0, in1, o, reads, writes):
        return op(eng, lambda e: e.tensor_tensor(out=out, in0=in0, in1=in1, op=o), reads=reads, writes=writes)

    def ts(eng, out, in0, s1, s2, o0, o1, reads, writes):
        if o1 is None:
            return op(eng, lambda e: e.tensor_scalar(out=out, in0=in0, scalar1=s1, scalar2=None, op0=o0),
                      reads=reads, writes=writes)
        return op(eng, lambda e: e.tensor_scalar(out=out, in0=in0, scalar1=s1, scalar2=s2, op0=o0, op1=o1),
                  reads=reads, writes=writes)

    def stt(out, in0, scalar, in1, o0, o1, reads, writes):
        return op('dve', lambda e: e.scalar_tensor_tensor(out=out, in0=in0, scalar=scalar, in1=in1, op0=o0, op1=o1),
                  reads=reads, writes=writes)

    def copy(eng, out, in_, reads, writes):
        if eng == 'act':
            return act(out, in_, AF.Copy, reads, writes)
        return op(eng, lambda e: e.tensor_copy(out=out, in_=in_), reads=reads, writes=writes)

    def memset(eng, ap, val, writes):
        return op(eng, lambda e: e.memset(ap, val), writes=writes)

    def transpose(out, in_, reads, writes, np_in=128):
        return op('pe', lambda e: e.transpose(out, in_, ident[:np_in, :np_in]), reads=list(reads) + ['ident'],
                  writes=writes)

    slabs = []
    slab_pos = [0]
    slab_issued = [0]

    def issue_upto(j):
        while slab_issued[0] <= j and slab_issued[0] < len(slabs):
            i = slab_issued[0]
            wname, col0, nk = slabs[i]
            slot = i % NSLOT
            src = Wd[wname].rearrange("(kc p) n -> p kc n", p=128)[:, :, col0:col0 + 128]
            if DEBUG.get('srcx'):
                src = x_d[0:1024, :].rearrange("(kc p) n -> p kc n", p=128)[:, :, col0:col0 + 128]
            dst = ring[:, slot, 0:nk * 128].rearrange("p (kc n) -> p kc n", n=128)
            dma('pool', dst, src, ('ring' if DEBUG.get('onesem') else 'ring%d' % slot), writes=[('ring', slot)])
            if DEBUG.get('serial'):
                op('pool', lambda e: e.memset(sc[:, 0:1], 0.0), reads=[('ring', slot)], writes=['scdummy'])
            slab_issued[0] += 1

    def next_slab(wname, col0, nk, ahead=LOOKAHEAD):
        i = slab_pos[0]
        assert slabs[i] == (wname, col0, nk), (i, slabs[i], wname, col0, nk)
        issue_upto(i + ahead)
        slab_pos[0] += 1
        return i % NSLOT

    def linear(jobs, nk, in_fn, nts, epilogue):
        PHASE[0] = "p%s:lin:%s:%d" % (CURP[0], jobs[0][0][0], jobs[0][0][1])
        for ji, job in enumerate(jobs):
            nj = len(job)
            slots = [next_slab(wn, cc * 128, nk, LOOKAHEAD - (nj - 1) - j) for j, (wn, cc) in enumerate(job)]
            if DEBUG.get('lin') == 'dma':
                continue
            for nt in nts:
                banks = []
                for slot in slots:
                    b = bank()
                    banks.append(b)
                    ins = [in_fn(kc, nt) for kc in range(nk)]

                    def fn(e, slot=slot, b=b, ins=ins, w=nt.w):
                        last = None
                        for kc in range(nk):
                            last = e.matmul(ps[:, b, :w], lhsT=ring[:, slot, kc * 128:(kc + 1) * 128],
                                            rhs=ins[kc][0], start=(kc == 0), stop=(kc == nk - 1))
                        return last
                    op('pe', fn, reads=[('ring', slot)] + [k for _, k in ins], writes=[('ps', b)])
                if DEBUG.get('lin') != 'mm':
                    epilogue(ji, nt, banks)

    def rstd_from_psum(b, w):
        ri = nxt('rs', NR)
        act(rs[:, ri, :w], ps[:, b, :w], AF.Ln, [('ps', b), 'consts'], [('rs', ri)], scale=1.0 / D, bias=epsb[:, 0:1])
        act(rs[:, ri, :w], rs[:, ri, :w], AF.Exp, [('rs', ri)], [('rs', ri)], scale=-0.5)
        return ri

    def warm(k):
        if k <= 0 or DEBUG.get('nowarm'):
            return

        def fn(e):
            last = None
            for _ in range(k):
                last = e.matmul(ps[:, 5, :], lhsT=ones_bf[:], rhs=wrm[:, :], start=True, stop=True)
            return last
        op('pe', fn, reads=['consts'], writes=[('ps', 5)])

    def norm_stats(src_fn, w):
        b = bank()
        for c in range(8):
            ap, key = src_fn(c)
            si = nxt('sq', 4)
            act(sq[:, si, :w], ap, AF.Square, [key], [('sq', si)])
            op('pe', lambda e, si=si, c=c: e.matmul(ps[:, b, :w], lhsT=ones_bf[:], rhs=sq[:, si, :w],
                                                   start=(c == 0), stop=(c == 7)),
               reads=[('sq', si), 'consts'], writes=[('ps', b)])
        return rstd_from_psum(b, w)

    def xkey(c, nt):
        return ('xT', c, nt.i)

    def prenorm(nt, gname):
        PHASE[0] = 'p%s:pre:%s' % (CURP[0], gname)
        gi = VI[gname]
        ri = norm_stats(lambda c: (xT[:, c, nt.cols], xkey(c, nt)), nt.w)
        if not nt.sample:
            warm(K_PRE)
        for c in range(8):
            stt(hT[:, c, nt.cols], xT[:, c, nt.cols], cst[:, c, gi:gi + 1], rs[:, ri, :nt.w], ALU.mult, ALU.mult,
                [xkey(c, nt), ('rs', ri), 'cst'], [('hT', c, nt.i)])

    def yb_ap(c, nt):
        if nt.sample:
            return ybs[:, c, :], ('ybs', c)
        return ybuf[:, c, nt.i, 4:4 + nt.w], ('ybuf', c, nt.i)

    def stat_bank(nt):
        return 6 if nt.sample else 7

    def y_epilogue(c, nt, b, gname):
        gi = GPOST[gname]
        ap, key = yb_ap(c, nt)
        w = nt.w
        sbk = stat_bank(nt)
        si = nxt('sq', 4)
        act(sq[:, si, :w], ps[:, b, :w], AF.Square, [('ps', b)], [('sq', si)])
        act(ap, ps[:, b, :w], AF.Copy, [('ps', b), 'cst'], [key], scale=cst[:, c, gi:gi + 1])
        op('pe', lambda e: e.matmul(ps[:, sbk, :w], lhsT=ones_bf[:], rhs=sq[:, si, :w], start=(c == 0), stop=(c == 7)),
           reads=[('sq', si), 'consts'], writes=[('ps', sbk)])

    def postnorm(nt, gname, coef):
        PHASE[0] = 'p%s:post:%s' % (CURP[0], gname)
        w = nt.w
        if not nt.sample:
            warm(K_POST)
        ri = rstd_from_psum(stat_bank(nt), w)
        for c in range(8):
            ap, key = yb_ap(c, nt)
            ti = nxt('tmp', NTMP)
            tt('dve', tmp[:, ti, :w], ap, rs[:, ri, :w], ALU.mult, [key, ('rs', ri)], [('tmp', ti)])
            tt('dve', xT[:, c, nt.cols], xT[:, c, nt.cols], tmp[:, ti, :w],
               ALU.add, [('tmp', ti), xkey(c, nt)], [xkey(c, nt)])

    def hT_in(kc, nt):
        return hT[:, kc, nt.cols], ('hT', kc, nt.i)

    def ffn(nts, pre, wg, wu, wd, gpre, gpost):
        for nt in nts:
            prenorm(nt, gpre)

        def ep1(m, nt, banks):
            bg, bu = banks
            ti = nxt('tmp', NTMP)
            act(tmp[:, ti, :nt.w], ps[:, bg, :nt.w], AF.Silu, [('ps', bg)], [('tmp', ti)])
            tt('dve', big[:, m, nt.cols], tmp[:, ti, :nt.w], ps[:, bu, :nt.w], ALU.mult,
               [('tmp', ti), ('ps', bu)], [('big', m, nt.i)])
        linear([[(wg, m), (wu, m)] for m in range(22)], 8, hT_in, nts, ep1)

        def ep2(c, nt, banks):
            y_epilogue(c, nt, banks[0], gpost)
        linear([[(wd, c)] for c in range(8)], 22, lambda kc, nt: (big[:, kc, nt.cols], ('big', kc, nt.i)), nts, ep2)
        for nt in nts:
            postnorm(nt, gpost, 0.5)

    def ffn_slabs(wg, wu, wd):
        out = []
        for m in range(22):
            out += [(wg, m * 128, 8), (wu, m * 128, 8)]
        out += [(wd, c * 128, 22) for c in range(8)]
        return out

    def R(i, nt):
        if nt.sample:
            return None
        if i < 8:
            return ybuf[:, i, nt.i, :], [('ybuf', i, nt.i)]
        ch = 8 + 2 * (i - 8)
        return (big[:, ch:ch + 2, :].rearrange("p a b -> p (a b)").bitcast(F32)[:, 0:516],
                [('big', ch, 0), ('big', ch + 1, 0)])

    def mix(nts, p):
        for nt in nts:
            prenorm(nt, "g_mix_pre")
        ptiles = [nt for nt in nts if not nt.sample]
        stiles = [nt for nt in nts if nt.sample]

        def c5(c, name):
            i = V5[name]
            return cst5[:, c, i:i + 1]

        def ep_xb(c, nt, banks):
            b = banks[0]
            if nt.sample:
                copy('act', Rs[:, 4 + c, :], ps[:, b, :NSMP], [('ps', b)], [('Rs', 4 + c)])
                return
            r, keys = R(4 + c, nt)
            copy('dve', r[:, 1:4], carryB[:, c, :], [('carryB', c)], keys)
            copy('act', r[:, 4:516], ps[:, b, :512], [('ps', b)], keys)
            copy('dve', carryB[:, c, :], r[:, 513:516], keys, [('carryB', c)])
        linear([[("w_in", 12 + c)] for c in range(4)], 8, hT_in, nts, ep_xb)

        def conv_b(c, nt):
            if nt.sample:
                o = Rs[:, 8 + c, :]
                ts('dve', o, SB_[:, c, 0, :], c5(c, "cbw0"), c5(c, "conv_b_b"), ALU.mult, ALU.add,
                   ['SB_', 'cst5'], [('Rs', 8 + c)])
                for k in (1, 2):
                    stt(o, SB_[:, c, k, :], c5(c, "cbw%d" % k), o, ALU.mult, ALU.add, ['SB_', 'cst5', ('Rs', 8 + c)],
                        [('Rs', 8 + c)])
                stt(o, Rs[:, 4 + c, :], c5(c, "cbw3"), o, ALU.mult, ALU.add, [('Rs', 4 + c), 'cst5', ('Rs', 8 + c)],
                    [('Rs', 8 + c)])
                return
            x_, xk = R(4 + c, nt)
            cbr, ck = R(10 + c, nt)
            o = cbr[:, 4:516]
            ts('dve', o, x_[:, 1:513], c5(c, "cbw0"), c5(c, "conv_b_b"), ALU.mult, ALU.add, xk + ['cst5'], ck)
            for k in (1, 2, 3):
                stt(o, x_[:, 1 + k:513 + k], c5(c, "cbw%d" % k), o, ALU.mult, ALU.add, xk + ck + ['cst5'], ck)

        for c in range(4):
            for nt in nts:
                conv_b(c, nt)

        def ep_A(which):
            def ep(c_rel, nt, banks):
                c = ep.c0 + c_rel // 3
                role = c_rel % 3
                b = banks[0]
                w = nt.w
                if nt.sample:
                    if role == 0:
                        copy('act', Rs[:, c, :], ps[:, b, :w], [('ps', b)], [('Rs', c)])
                    elif role == 1:
                        tt('dve', Rs[:, c, :], Rs[:, c, :], ps[:, b, :w], ALU.mult, [('Rs', c), ('ps', b)], [('Rs', c)])
                        o = Rs[:, 12 + c, :]
                        ts('dve', o, SA[:, c, 0, :], c5(c, "caw0"), None, ALU.mult, None, ['SA', 'cst5'],
                           [('Rs', 12 + c)])
                        stt(o, SA[:, c, 1, :], c5(c, "caw1"), o, ALU.mult, ALU.add, ['SA', 'cst5', ('Rs', 12 + c)],
                            [('Rs', 12 + c)])
                        stt(o, Rs[:, c, :], c5(c, "caw2"), o, ALU.mult, ALU.add, [('Rs', c), 'cst5', ('Rs', 12 + c)],
                            [('Rs', 12 + c)])
                    else:
                        tt('dve', big[:, c, nt.cols], Rs[:, 12 + c, :], ps[:, b, :w], ALU.mult,
                           [('Rs', 12 + c), ('ps', b)], [('big', c, nt.i)])
                    return
                v_, vk = R(c, nt)
                ca_, cak = R(8 + (c % 2), nt)
                if role == 0:
                    copy('dve', v_[:, 2:4], carryA[:, c, :], [('carryA', c)], vk)
                    copy('act', v_[:, 4:516], ps[:, b, :512], [('ps', b)], vk)
                elif role == 1:
                    tt('dve', v_[:, 4:516], v_[:, 4:516], ps[:, b, :512], ALU.mult, vk + [('ps', b)], vk)
                    copy('dve', carryA[:, c, :], v_[:, 514:516], vk, [('carryA', c)])
                    o = ca_[:, 4:516]
                    ts('dve', o, v_[:, 2:514], c5(c, "caw0"), None, ALU.mult, None, vk + ['cst5'], cak)
                    stt(o, v_[:, 3:515], c5(c, "caw1"), o, ALU.mult, ALU.add, vk + cak + ['cst5'], cak)
                    stt(o, v_[:, 4:516], c5(c, "caw2"), o, ALU.mult, ALU.add, vk + cak + ['cst5'], cak)
                else:
                    tt('dve', big[:, c, nt.cols], ca_[:, 4:516], ps[:, b, :512], ALU.mult, cak + [('ps', b)],
                       [('big', c, nt.i)])
            ep.c0 = which
            return ep

        def A_jobs(c0, n):
            jobs = []
            for c in range(c0, c0 + n):
                jobs += [[("w_in", 8 + c)], [("w_in", 4 + c)], [("w_in", c)]]
            return jobs
        linear(A_jobs(0, 2), 8, hT_in, nts, ep_A(0))

        def lru(c, nt):
            PHASE[0] = 'p%s:lru' % CURP[0]
            w = nt.w
            if nt.sample:
                cbv = Rs[:, 8 + c, :]
                cbk = [('Rs', 8 + c)]
            else:
                cbr, cbk = R(10 + c, nt)
                cbv = cbr[:, 4:516]
            ci = nxt('cb16', 2)
            copy('act', cb16[:, ci, :w], cbv, cbk, [('cb16', ci)])
            ba, bx = bank(), bank()
            op('pe', lambda e: e.matmul(ps[:, ba, :w], lhsT=bd[:, c, :], rhs=cb16[:, ci, :w], start=True, stop=True),
               reads=[('cb16', ci), 'bd'], writes=[('ps', ba)])
            op('pe', lambda e: e.matmul(ps[:, bx, :w], lhsT=bd[:, 4 + c, :], rhs=cb16[:, ci, :w], start=True, stop=True),
               reads=[('cb16', ci), 'bd'], writes=[('ps', bx)])
            t_a, t_x, t_aa, t_m = [nxt('tmp', NTMP) for _ in range(4)]
            act(tmp[:, t_a, :w], ps[:, ba, :w], AF.Tanh, [('ps', ba), 'c_lru'], [('tmp', t_a)],
                scale=0.5, bias=c_lru[:, c, 2:3])
            act(tmp[:, t_x, :w], ps[:, bx, :w], AF.Tanh, [('ps', bx), 'c_lru'], [('tmp', t_x)],
                scale=0.5, bias=c_lru[:, c, 3:4])
            act(tmp[:, t_aa, :w], tmp[:, t_a, :w], AF.Exp, [('tmp', t_a), 'c_lru'], [('tmp', t_aa)],
                scale=c_lru[:, c, 0:1], bias=c_lru[:, c, 0:1])
            act(tmp[:, t_m, :w], tmp[:, t_a, :w], AF.Exp, [('tmp', t_a), 'c_lru'], [('tmp', t_m)],
                scale=c_lru[:, c, 1:2], bias=c_lru[:, c, 1:2])
            ts('dve', tmp[:, t_m, :w], tmp[:, t_m, :w], -1.0, 1.0, ALU.mult, ALU.add, [('tmp', t_m)], [('tmp', t_m)])
            act(tmp[:, t_m, :w], tmp[:, t_m, :w], AF.Sqrt, [('tmp', t_m)], [('tmp', t_m)])
            if (not nt.sample) and p == 0 and nt.i == 0:
                memset('dve', tmp[:, t_m, 0:1], 1.0, [('tmp', t_m)])
            stt(tmp[:, t_x, :w], tmp[:, t_x, :w], 1.0, cbv, ALU.add, ALU.mult, [('tmp', t_x)] + cbk, [('tmp', t_x)])
            stt(tmp[:, t_x, :w], tmp[:, t_x, :w], 0.5, tmp[:, t_m, :w], ALU.mult, ALU.mult,
                [('tmp', t_x), ('tmp', t_m)], [('tmp', t_x)])
            if nt.sample:
                hb = Rs[:, 16 + c, :]
                tt('dve', hb, tmp[:, t_aa, :w], SH[:, c, :], ALU.mult, [('tmp', t_aa), 'SH'], [('Rs', 16 + c)])
                tt('dve', hb, hb, tmp[:, t_x, :w], ALU.add, [('Rs', 16 + c), ('tmp', t_x)], [('Rs', 16 + c)])
            else:
                hbr, hk = R(10 + c, nt)
                op('dve', lambda e: e.tensor_tensor_scan(out=hbr[:, 4:516], data0=tmp[:, t_aa, :w],
                                                         data1=tmp[:, t_x, :w], initial=hcar[:, c, :],
                                                         op0=ALU.mult, op1=ALU.add),
                   reads=[('tmp', t_aa), ('tmp', t_x), ('hcar', c)], writes=hk)
                copy('dve', hcar[:, c, :], hbr[:, 515:516], hk, [('hcar', c)])

        for c in range(4):
            for nt in nts:
                lru(c, nt)

        linear(A_jobs(2, 2), 8, hT_in, nts, ep_A(2))

        def ep_gg(c, nt, banks):
            b = banks[0]
            w = nt.w
            g = ps[:, b, :w]
            if nt.sample:
                hv, hk = Rs[:, 16 + c, :], [('Rs', 16 + c)]
            else:
                hbr, hk = R(10 + c, nt)
                hv = hbr[:, 4:516]
            t1, t2 = nxt('tmp', NTMP), nxt('tmp', NTMP)
            act(tmp[:, t1, :w], g, AF.Square, [('ps', b)], [('tmp', t1)], scale=float(np.sqrt(C1)))
            stt(tmp[:, t1, :w], tmp[:, t1, :w], 1.0, g, ALU.add, ALU.mult, [('tmp', t1), ('ps', b)], [('tmp', t1)])
            act(tmp[:, t2, :w], tmp[:, t1, :w], AF.Tanh, [('tmp', t1)], [('tmp', t2)], scale=C0)
            stt(tmp[:, t2, :w], tmp[:, t2, :w], 1.0, g, ALU.add, ALU.mult, [('tmp', t2), ('ps', b)], [('tmp', t2)])
            stt(big[:, 4 + c, nt.cols], tmp[:, t2, :w], 0.5, hv, ALU.mult, ALU.mult, [('tmp', t2)] + hk,
                [('big', 4 + c, nt.i)])
        linear([[("w_in", 16 + c)] for c in range(4)], 8, hT_in, nts, ep_gg)

        def ep2(c, nt, banks):
            y_epilogue(c, nt, banks[0], "g_mix_post")
        linear([[("w_out", c)] for c in range(8)], 8, lambda kc, nt: (big[:, kc, nt.cols], ('big', kc, nt.i)), nts, ep2)
        for nt in nts:
            postnorm(nt, "g_mix_post", 1.0)

    def mix_slabs():
        out = [("w_in", (12 + c) * 128, 8) for c in range(4)]
        for c in range(4):
            out += [("w_in", (8 + c) * 128, 8), ("w_in", (4 + c) * 128, 8), ("w_in", c * 128, 8)]
        out += [("w_in", (16 + c) * 128, 8) for c in range(4)]
        out += [("w_out", c * 128, 8) for c in range(8)]
        return out

    def prompt_attn(nt):
        PHASE[0] = 'p%s:pattn' % CURP[0]
        w = nt.w
        for h in range(4):
            pi = nxt('pT', 2)
            for mc in range(2):
                b = bank()

                def fn(e, b=b, mc=mc, h=h):
                    e.matmul(ps[:, b, :w], lhsT=KT[:, 2 * h, mc * 128:(mc + 1) * 128], rhs=big[:, 2 * h, nt.cols],
                             start=True, stop=False)
                    return e.matmul(ps[:, b, :w], lhsT=KT[:, 2 * h + 1, mc * 128:(mc + 1) * 128],
                                    rhs=big[:, 2 * h + 1, nt.cols], start=False, stop=True)
                op('pe', fn, reads=['KT', ('big', 2 * h, nt.i), ('big', 2 * h + 1, nt.i)], writes=[('ps', b)])
                act(pT[:, pi, mc, :w], ps[:, b, :w], AF.Exp, [('ps', b)], [('pT', pi, mc)], scale=1.0 / 16.0)
            bden = bank()

            def fnd(e, bden=bden, pi=pi):
                e.matmul(ps[:, bden, :w], lhsT=ones_bf[:], rhs=pT[:, pi, 0, :w], start=True, stop=False)
                return e.matmul(ps[:, bden, :w], lhsT=ones_bf[:], rhs=pT[:, pi, 1, :w], start=False, stop=True)
            op('pe', fnd, reads=[('pT', pi, 0), ('pT', pi, 1), 'consts'], writes=[('ps', bden)])
            act(rden[:, 0, :w], ps[:, bden, :w], AF.Ln, [('ps', bden)], ['rden'])
            act(rden[:, 0, :w], rden[:, 0, :w], AF.Exp, ['rden'], ['rden'], scale=-1.0)
            for ee in range(2):
                bo = bank()
                cc = 2 * h + ee

                def fno(e, bo=bo, cc=cc, pi=pi):
                    e.matmul(ps[:, bo, :w], lhsT=Vn[:, 0, cc * 128:(cc + 1) * 128], rhs=pT[:, pi, 0, :w],
                             start=True, stop=False)
                    return e.matmul(ps[:, bo, :w], lhsT=Vn[:, 1, cc * 128:(cc + 1) * 128], rhs=pT[:, pi, 1, :w],
                                    start=False, stop=True)
                op('pe', fno, reads=['Vn', ('pT', pi, 0), ('pT', pi, 1)], writes=[('ps', bo)])
                tt('dve', big[:, 8 + cc, nt.cols], ps[:, bo, :w], rden[:, 0, :w], ALU.mult,
                   [('ps', bo), 'rden'], [('big', 8 + cc, nt.i)])

    def xattn(nts, p):
        for nt in nts:
            prenorm(nt, "g_xattn_pre")

        def ep_q(c, nt, banks):
            b = banks[0]
            copy('act', big[:, c, nt.cols], ps[:, b, :nt.w], [('ps', b)], [('big', c, nt.i)])
            if nt.sample:
                copy('dve', qs32[:, c, :], ps[:, b, :nt.w], [('ps', b)], [('qs32', c)])
        linear([[("xattn_wq", c)] for c in range(8)], 8, hT_in, nts, ep_q)

        for nt in nts:
            if nt.sample:
                sample_attn(nt)
            else:
                prompt_attn(nt)

        def ep2(c, nt, banks):
            y_epilogue(c, nt, banks[0], "g_xattn_post")
        linear([[("xattn_wo", c)] for c in range(8)], 8,
               lambda kc, nt: (big[:, 8 + kc, nt.cols], ('big', 8 + kc, nt.i)), nts, ep2)
        for nt in nts:
            postnorm(nt, "g_xattn_post", 1.0)

    def xattn_slabs():
        return [("xattn_wq", c * 128, 8) for c in range(8)] + [("xattn_wo", c * 128, 8) for c in range(8)]

    def sample_attn(nt):
        PHASE[0] = 'p%s:sattn' % CURP[0]
        for half in range(2):
            b = bank()
            for j in range(4):
                c = half * 4 + j
                transpose(ps[:NSMP, b, j * 128:(j + 1) * 128], qs32[:, c, :], [('qs32', c)], [('ps', b)])
            copy('act', qtok[:, half * 512:(half + 1) * 512], ps[:NSMP, b, :], [('ps', b)], [('qtok', half)])
        BO = 7
        for s in range(NSMP):
            ri = s % 2
            src_k = ck_d[s].rearrange("(mc p) d -> p mc d", p=128)
            src_v = cv_d[s].rearrange("(mc p) d -> p mc d", p=128)
            dma('pool', Ks[:, ri, :, :], src_k, 'ks%d' % ri, writes=[('Ks', ri)])
            dma('pool', Vs[:, ri, :, :], src_v, 'vs%d' % ri, writes=[('Vs', ri)])
            b0 = nxt('bank', NBANK)
            while b0 == NBANK - 1:
                b0 = nxt('bank', NBANK)
            b1 = nxt('bank', NBANK)
            assert b1 == b0 + 1
            for hh, bb in ((0, b0), (1, b1)):
                op('pe', lambda e, hh=hh, bb=bb, s=s: e.matmul(ps[:, bb, :], lhsT=sel[:, s, :],
                                                              rhs=qtok[:, hh * 512:(hh + 1) * 512],
                                                              start=True, stop=True),
                   reads=[('qtok', hh), 'sel'], writes=[('ps', bb)])
            qbc = ps[:, b0:b0 + 2, :].rearrange("p a b -> p (a b)")
            for mc in range(2):
                tt('dve', prod, Ks[:, ri, mc, :], qbc, ALU.mult, [('Ks', ri), ('ps', b0), ('ps', b1)], PRODK)
                op('dve', lambda e, mc=mc: e.tensor_reduce(out=sc[:, mc * 4:(mc + 1) * 4],
                                                          in_=prod.rearrange("p (h d) -> p h d", d=256),
                                                          axis=AX.X, op=ALU.add),
                   reads=PRODK, writes=[('sc', mc)])
            ei = nxt('e16', 2)
            act(e16[:, ei, :], sc[:, :], AF.Exp, [('sc', 0), ('sc', 1)], [('e16', ei)], scale=1.0 / 16.0)
            bden = bank()
            op('pe', lambda e, bden=bden, ei=ei: e.matmul(ps[:, bden, 0:8], lhsT=ones_bf[:], rhs=e16[:, ei, :],
                                                         start=True, stop=True),
               reads=[('e16', ei), 'consts'], writes=[('ps', bden)])
            op('dve', lambda e, bden=bden, s=s: e.tensor_reduce(
                out=rdens[:, s, :], in_=ps[:, bden, 0:8].rearrange("p (mc h) -> p h mc", mc=2),
                axis=AX.X, op=ALU.add), reads=[('ps', bden)], writes=[('rdens', s)])
            op('dve', lambda e, s=s: e.reciprocal(out=rdens[:, s, :], in_=rdens[:, s, :]),
               reads=[('rdens', s)], writes=[('rdens', s)])

            def fpv(e, s=s, ri=ri, ei=ei):
                last = None
                for c in range(8):
                    h = c // 2
                    for mc in range(2):
                        last = e.matmul(ps[:, BO, c * NSMP + s:c * NSMP + s + 1],
                                        lhsT=Vs[:, ri, mc, c * 128:(c + 1) * 128],
                                        rhs=e16[:, ei, mc * 4 + h:mc * 4 + h + 1], start=(mc == 0), stop=(mc == 1))
                return last
            op('pe', fpv, reads=[('Vs', ri), ('e16', ei)], writes=[('ps', BO)])
        for c in range(8):
            h = c // 2
            tt('dve', big[:, 8 + c, nt.cols], ps[:, BO, c * NSMP:(c + 1) * NSMP], rdens[:, :, h], ALU.mult,
               [('ps', BO)] + [('rdens', s) for s in range(NSMP)], [('big', 8 + c, nt.i)])

    XK0, XK1 = ('xin', 0), ('xin', 1)

    def setup():
        memset('dve', ones32[:], 1.0, ['ones32'])
        op('pool', lambda e: e.affine_select(out=ident[:], in_=ones32[:], pattern=[[1, 128]],
                                             compare_op=ALU.is_equal, fill=0.0, base=0, channel_multiplier=-1),
           reads=['ones32'], writes=['ident'])
        memset('dve', ones_bf[:], 1.0, ['consts'])
        memset('dve', epsb[:], EPS, ['consts'])
        memset('dve', wrm[:], 0.37, ['consts'])
        memset('dve', carryA[:], 0.0, [('carryA', c) for c in range(4)])
        memset('dve', carryB[:], 0.0, [('carryB', c) for c in range(4)])
        memset('dve', hcar[:], 0.0, [('hcar', c) for c in range(4)])
        memset('dve', bd[:], 0.0, ['bd'])
        memset('dve', xin[0:32, 0, :], 0.0, [XK0])
        memset('dve', xin[0:32, 1, :], 0.0, [XK1])
        for n in vec1024:
            dma('sp', vstage[VI[n]:VI[n] + 1, :], vec_d[n][0:1, :], 'cload', writes=[XK0])
        dma('sp', v5stage[0:3, :], caw_d[:, :], 'cload5', writes=[XK1])
        dma('sp', v5stage[3:7, :], cbw_d[:, :], 'cload5', writes=[XK1])
        for n in v512:
            dma('sp', v5stage[V5[n]:V5[n] + 1, :], v512_d[n][0:1, :], 'cload5', writes=[XK1])
        for g, wd_ in ((0, lwa_d), (1, lwx_d)):
            if DEBUG.get('nobd'):
                break
            for h in range(8):
                c, r = h // 2, h % 2
                dma('pool', bd[r * 64:(r + 1) * 64, g * 4 + c, r * 64:(r + 1) * 64], wd_[h], 'bdload',
                    writes=['bd'])
        for half in range(2):
            b = bank()
            for j in range(4):
                c = half * 4 + j
                transpose(ps[:, b, j * 32:(j + 1) * 32], vstage[:, c * 128:(c + 1) * 128], [XK0], [('ps', b)],
                          np_in=32)
            copy('dve', cst[:, half * 4:(half + 1) * 4, :], ps[:, b, 0:128].rearrange("p (j v) -> p j v", v=32),
                 [('ps', b)], ['cst'])
        b = bank()
        for c in range(4):
            transpose(ps[:, b, c * 32:(c + 1) * 32], v5stage[:, c * 128:(c + 1) * 128], [XK1], [('ps', b)],
                      np_in=32)
        copy('dve', cst5[:, :, :], ps[:, b, 0:128].rearrange("p (j v) -> p j v", v=32), [('ps', b)], ['cst5'])
        ts('dve', cst[:, :, 16], cst[:, :, VI["g_ffn1_post"]], 0.5, None, ALU.mult, None, ['cst'], ['cst'])
        ts('dve', cst[:, :, 17], cst[:, :, VI["g_ffn2_post"]], 0.5, None, ALU.mult, None, ['cst'], ['cst'])
        il, iba, ibx = V5["lru_lam"], V5["lru_ba"], V5["lru_bx"]
        K = ['c_lru']
        act(c_lru[:, :, 0], cst5[:, :, il], AF.Exp, ['cst5'], K, scale=-1.0)
        ts('dve', c_lru[:, :, 1], c_lru[:, :, 0], 1.0 / 3.0, -0.5, ALU.mult, ALU.add, K, K)
        tt('dve', c_lru[:, :, 1], c_lru[:, :, 1], c_lru[:, :, 0], ALU.mult, K, K)
        ts('dve', c_lru[:, :, 1], c_lru[:, :, 1], 1.0, None, ALU.add, None, K, K)
        tt('dve', c_lru[:, :, 1], c_lru[:, :, 1], c_lru[:, :, 0], ALU.mult, K, K)
        ts('dve', c_lru[:, :, 0], c_lru[:, :, 1], -4.0, None, ALU.mult, None, K, K)
        ts('dve', c_lru[:, :, 1], c_lru[:, :, 1], -8.0, None, ALU.mult, None, K, K)
        ts('dve', c_lru[:, :, 2], cst5[:, :, iba], 0.5, None, ALU.mult, None, ['cst5'] + K, K)
        ts('dve', c_lru[:, :, 3], cst5[:, :, ibx], 0.5, None, ALU.mult, None, ['cst5'] + K, K)
        copy('dve', sel[:, :, :], ident[:NSMP, :NSMP].unsqueeze(2).to_broadcast([NSMP, NSMP, 128]), ['ident'], ['sel'])

    def prefetch_x(p):
        for tcn in range(2):
            t0 = p * PT + tcn * 128
            dma('sp', xpre[:, tcn, :], x_d[t0:t0 + 128, :], 'xpre%d' % tcn, writes=[('xpre', tcn)])

    def load_x(p, nt):
        PHASE[0] = 'p%s:load' % p
        for tcn in range(PT // 128):
            t0 = p * PT + tcn * 128
            if tcn < 2:
                src, skey = xpre[:, tcn, :], ('xpre', tcn)
            else:
                xi = nxt('xin', 2)
                dma('sp', xin[:, xi, :], x_d[t0:t0 + 128, :], 'xin%d' % xi, writes=[('xin', xi)])
                src, skey = xin[:, xi, :], ('xin', xi)
            col = tcn * 128
            for half in range(2):
                b = bank()
                for j in range(4):
                    c = half * 4 + j
                    transpose(ps[:, b, j * 128:(j + 1) * 128], src[:, c * 128:(c + 1) * 128], [skey],
                              [('ps', b)])
                copy('act' if half == 0 else 'dve', xT[:, half * 4:(half + 1) * 4, col:col + 128],
                     ps[:, b, :].rearrange("p (j t) -> p j t", t=128), [('ps', b)],
                     [('xT', c, nt.i) for c in range(half * 4, half * 4 + 4)])

    def load_samples(nt):
        dma('sp', xin[0:NSMP, 0, :], xs_d[:, :], 'xin0', writes=[XK0])
        for half in range(2):
            b = bank()
            for j in range(4):
                c = half * 4 + j
                transpose(ps[:, b, j * NSMP:(j + 1) * NSMP], xin[0:NSMP, 0, c * 128:(c + 1) * 128], [XK0], [('ps', b)],
                          np_in=NSMP)
            copy('dve', xT[:, half * 4:(half + 1) * 4, nt.cols],
                 ps[:, b, 0:4 * NSMP].rearrange("p (j t) -> p j t", t=NSMP), [('ps', b)],
                 [('xT', c, nt.i) for c in range(half * 4, half * 4 + 4)])
        jobs = [(sca_d, 0, SA[:, :, 0, :], 'SA'), (sca_d, 1, SA[:, :, 1, :], 'SA'),
                (scb_d, 0, SB_[:, :, 0, :], 'SB_'), (scb_d, 1, SB_[:, :, 1, :], 'SB_'), (scb_d, 2, SB_[:, :, 2, :], 'SB_'),
                (slru_d, 0, SH[:, :, :], 'SH')]
        for (src, k, dst, key) in jobs:
            xi = nxt('xin', 2)
            dma('sp', xin[0:NSMP, xi, 0:512], src[:, k * 512:(k + 1) * 512], 'xin%d' % xi, writes=[('xin', xi)])
            b = bank()
            for c in range(4):
                transpose(ps[:, b, c * NSMP:(c + 1) * NSMP], xin[0:NSMP, xi, c * 128:(c + 1) * 128], [('xin', xi)],
                          [('ps', b)], np_in=NSMP)
            copy('dve', dst, ps[:, b, 0:4 * NSMP].rearrange("p (c t) -> p c t", t=NSMP), [('ps', b)], [key])

    def memory_kv():
        PHASE[0] = 'memkv'
        gi = VI["g_mem"]
        for j in range(2):
            if DEBUG.get('nomem1'):
                break
            dma('sp', xin[:, j, :], mem_d[j * 128:(j + 1) * 128, :], 'xin%d' % j, writes=[('xin', j)])
            ri = nxt('rs', NR)
            op('act', lambda e, j=j, ri=ri: e.activation(out=prod, in_=xin[:, j, :], func=AF.Square,
                                                        accum_out=rs[:, ri, 0:1]),
               reads=[('xin', j)], writes=PRODK + [('rs', ri)])
            ts('dve', rs[:, ri, 0:1], rs[:, ri, 0:1], 1.0 / D, EPS, ALU.mult, ALU.add, [('rs', ri)], [('rs', ri)])
            act(rs[:, ri, 0:1], rs[:, ri, 0:1], AF.Sqrt, [('rs', ri)], [('rs', ri)])
            op('dve', lambda e, ri=ri: e.reciprocal(out=rs[:, ri, 1:2], in_=rs[:, ri, 0:1]), reads=[('rs', ri)], writes=[('rs', ri)])
            ts('dve', xin[:, j, :], xin[:, j, :], rs[:, ri, 1:2], None, ALU.mult, None, [('xin', j), ('rs', ri)],
               [('xin', j)])
            for half in range(2):
                b = bank()
                for jj in range(4):
                    c = half * 4 + jj
                    transpose(ps[:, b, jj * 128:(jj + 1) * 128], xin[:, j, c * 128:(c + 1) * 128], [('xin', j)],
                              [('ps', b)])
                for jj in range(4):
                    c = half * 4 + jj
                    mt, mkey = memT(c)
                    ts('dve', mt[:, j * 128:(j + 1) * 128], ps[:, b, jj * 128:(jj + 1) * 128],
                       cst[:, c, gi:gi + 1], None, ALU.mult, None, [('ps', b), 'cst'], [mkey])
        if DEBUG.get('mkv') == 'A':
            slab_pos[0] = 16
            slab_issued[0] = 16
            return
        mnt = NT(0, 0, NM)

        def in_fn(kc, nt):
            return memT(kc)

        for (wn, out_d, isv) in (("xattn_wk", mk_d, False), ("xattn_wv", mv_d, True)):
            def ep(c, nt, banks, isv=isv):
                b = banks[0]
                k32, kkey = kt32(c)
                copy('act', k32, ps[:, b, :NM], [('ps', b)], [kkey])
                if not isv:
                    copy('dve', KT[:, c, :], ps[:, b, :NM], [('ps', b)], ['KT'])
            linear([[(wn, c)] for c in range(8)], 8, in_fn, [mnt], ep)
            if DEBUG.get('mkv') == 'B':
                continue
            for j in range(2):
                oi = nxt('xin', 2)
                for half in range(2):
                    b = bank()
                    for jj in range(4):
                        c = half * 4 + jj
                        k32, kkey = kt32(c)
                        transpose(ps[:, b, jj * 128:(jj + 1) * 128], k32[:, j * 128:(j + 1) * 128], [kkey], [('ps', b)])
                    copy('act', xin[:, oi, half * 512:(half + 1) * 512], ps[:, b, :], [('ps', b)], [('xin', oi)])
                    if isv:
                        copy('dve', Vn[:, j, half * 512:(half + 1) * 512], ps[:, b, :], [('ps', b)], ['Vn'])
                dma('sp', out_d[j * 128:(j + 1) * 128, :], xin[:, oi, :], 'xin%d' % oi, reads=[('xin', oi)])

    def memkv_slabs():
        return [("xattn_wk", c * 128, 8) for c in range(8)] + [("xattn_wv", c * 128, 8) for c in range(8)]

    def store_y(p, nt):
        PHASE[0] = 'p%s:store' % p
        for tcn in range(PT // 128):
            oi = nxt('xin', 2)
            col = tcn * 128
            for half in range(2):
                b = bank()
                for j in range(4):
                    c = half * 4 + j
                    transpose(ps[:, b, j * 128:(j + 1) * 128], xT[:, c, col:col + 128], [('xT', c, nt.i)], [('ps', b)])
                copy('act' if half == 0 else 'dve', xin[:, oi, half * 512:(half + 1) * 512], ps[:, b, :],
                     [('ps', b)], [('xin', oi)])
            t0 = p * PT + tcn * 128
            dma('sp', y_d[t0:t0 + 128, :], xin[:, oi, :], 'xin%d' % oi, reads=[('xin', oi)])

    def store_samples(nt):
        oi = nxt('xin', 2)
        for half in range(2):
            b = bank()
            for j in range(4):
                c = half * 4 + j
                transpose(ps[:NSMP, b, j * 128:(j + 1) * 128], xT[:, c, nt.cols], [('xT', c, nt.i)], [('ps', b)])
            copy('dve', xin[0:NSMP, oi, half * 512:(half + 1) * 512], ps[:NSMP, b, :], [('ps', b)], [('xin', oi)])
        dma('sp', ys_d[:, :], xin[0:NSMP, oi, :], 'xin%d' % oi, reads=[('xin', oi)])

    def store_sample_states():
        for base, dst in ((0, cas_d[:, 512:1024]), (4, cbs_d[:, 1024:1536]), (16, hs_d[:, :])):
            oi = nxt('xin', 2)
            b = bank()
            for c in range(4):
                transpose(ps[:NSMP, b, c * 128:(c + 1) * 128], Rs[:, base + c, :], [('Rs', base + c)], [('ps', b)])
            copy('dve', xin[0:NSMP, oi, 0:512], ps[:NSMP, b, :], [('ps', b)], [('xin', oi)])
            dma('sp', dst, xin[0:NSMP, oi, 0:512], 'xin%d' % oi, reads=[('xin', oi)])
        dma('sp', cas_d[:, 0:512], sca_d[:, 512:1024], 'ostore')
        dma('sp', cbs_d[:, 0:1024], scb_d[:, 512:1536], 'ostore')

    def store_prompt_states():
        for c in range(4):
            for (dst, src, key) in ((cap_d, carryA, 'carryA'), (cbp_d, carryB, 'carryB'), (hp_d, hcar, 'hcar')):
                op('sp', lambda e, c=c, dst=dst, src=src: e.dma_start(
                    out=dst[:, c * 128:(c + 1) * 128].rearrange("k p -> p k"), in_=src[:, c, :],
                    allow_slow_non_contiguous=True), reads=[(key, c)], dma_sem='ostore')
        dma_sems.add('ostore')

    stages = ['ffn1', 'mix', 'xattn', 'ffn2']
    nstage = len(stages) if stop is None else (stages.index(stop) + 1 if stop in stages else 0)
    per_pass = []
    if nstage >= 1:
        per_pass += ffn_slabs("ffn1_wg", "ffn1_wu", "ffn1_wd")
    if nstage >= 2:
        per_pass += mix_slabs()
    if nstage >= 3:
        per_pass += xattn_slabs()
    if nstage >= 4:
        per_pass += ffn_slabs("ffn2_wg", "ffn2_wu", "ffn2_wd")
    slabs.extend(memkv_slabs())
    for p in range(NPASS):
        slabs.extend(per_pass)

    if not DEBUG.get('nosetup'):
        setup()
    if stop != 'io0':
        memory_kv()
    else:
        slab_pos[0] = 16
        slab_issued[0] = 16
    for p in range(NPASS):
        if DEBUG.get('nopass'):
            break
        CURP[0] = p
        pnt = NT(0, 0, PT)
        nts = [pnt]
        last = (p == NPASS - 1)
        if last:
            snt = NT(1, PT, NSMP, sample=True)
            nts.append(snt)
        if p == 0:
            prefetch_x(0)
        load_x(p, pnt)
        if last:
            load_samples(snt)
        if nstage >= 1:
            ffn(nts, None, "ffn1_wg", "ffn1_wu", "ffn1_wd", "g_ffn1_pre", "g_ffn1_post")
        if nstage >= 2:
            mix(nts, p)
        if nstage >= 3:
            xattn(nts, p)
        if not last:
            prefetch_x(p + 1)
        if nstage >= 4:
            ffn(nts, None, "ffn2_wg", "ffn2_wu", "ffn2_wd", "g_ffn2_pre", "g_ffn2_post")
        store_y(p, pnt)
        if last:
            store_samples(snt)
            if nstage >= 2:
                store_sample_states()
    if nstage >= 2:
        store_prompt_states()
    assert slab_pos[0] == len(slabs), (slab_pos[0], len(slabs))

    semh = {}
    for name in list(S.ENG) + sorted(dma_sems):
        semh[name] = es.enter_context(nc.semaphore(name))
    hw = {'pe': 'tensor', 'act': 'scalar', 'dve': 'vector', 'pool': 'gpsimd', 'sp': 'sync'}

    def replay(name, e):
        for waits, fn, inc, ph in S.streams[name]:
            for s, v in waits:
                e.wait_ge(semh[s], v)
            ins = fn(e)
            if DEBUG.get('annot'):
                ins.annotate(ph)
            ins.then_inc(semh[inc[0]], inc[1])
        if name == 'sp':
            for s in sorted(dma_sems):
                if S.count.get(s, 0) > 0:
                    e.wait_ge(semh[s], S.count[s])
            for s in ('pe', 'act', 'dve', 'pool'):
                if S.count.get(s, 0) > 0:
                    e.wait_ge(semh[s], S.count[s])

    with nc.Block() as block:
        @block.tensor
        def _(e):
            replay('pe', e)

        @block.scalar
        def _(e):
            replay('act', e)

        @block.vector
        def _(e):
            replay('dve', e)

        @block.gpsimd
        def _(e):
            replay('pool', e)

        @block.sync
        def _(e):
            replay('sp', e)
    es.close()
    return nc


_W_NAMES = ["ffn1_wg", "ffn1_wu", "ffn1_wd", "w_in", "w_out", "xattn_wq", "xattn_wk", "xattn_wv", "xattn_wo",
            "ffn2_wg", "ffn2_wu", "ffn2_wd"]
_V_NAMES = ["g_ffn1_pre", "g_ffn1_post", "g_mix_pre", "g_mix_post", "g_xattn_pre", "g_xattn_post", "g_mem",
            "g_ffn2_pre", "g_ffn2_post", "conv_b_b", "lru_ba", "lru_bx", "lru_lam"]


def make_in_maps(inputs):
    f = lambda a: np.ascontiguousarray(np.asarray(a, dtype=np.float32))
    shared = {}
    for n in _W_NAMES:
        shared[n] = f(inputs[n][0])
    for n in _V_NAMES:
        shared[n] = f(inputs[n][0]).reshape(1, -1)
    shared["conv_a_w"] = f(inputs["conv_a_w"][0])
    shared["conv_b_w"] = f(inputs["conv_b_w"][0])
    shared["lru_wa"] = f(inputs["lru_wa"][0])
    shared["lru_wx"] = f(inputs["lru_wx"][0])
    maps = []
    for b in range(8):
        m = dict(shared)
        sl = slice(b * NSMP, (b + 1) * NSMP)
        m["x"] = f(inputs["x_prompt"][b])
        m["xs"] = f(inputs["x_sample"][sl, 0, :])
        m["mem"] = f(inputs["mem_prompt"][b])
        m["ck"] = f(inputs["cache_mem_k"][0, sl]).reshape(NSMP, NM, D)
        m["cv"] = f(inputs["cache_mem_v"][0, sl]).reshape(NSMP, NM, D)
        m["sca"] = f(inputs["state_conv_a"][0, sl]).reshape(NSMP, 1024)
        m["scb"] = f(inputs["state_conv_b"][0, sl]).reshape(NSMP, 1536)
        m["slru"] = f(inputs["state_lru"][0, sl]).reshape(NSMP, 512)
        maps.append(m)
    return maps


def assemble(results):
    cat = lambda k: np.stack([r[k] for r in results], axis=0)
    yp = cat("y")
    ys = np.concatenate([r["ys"] for r in results], axis=0).reshape(128, 1, D)
    mk = cat("mk").reshape(1, 8, NM, 4, 256)
    mv = cat("mv").reshape(1, 8, NM, 4, 256)
    cap = cat("cap").reshape(1, 8, 2, 512)
    cbp = cat("cbp").reshape(1, 8, 3, 512)
    hp = cat("hp").reshape(1, 8, 512)
    cas = np.concatenate([r["cas"] for r in results], axis=0).reshape(1, 128, 2, 512)
    cbs = np.concatenate([r["cbs"] for r in results], axis=0).reshape(1, 128, 3, 512)
    hs = np.concatenate([r["hs"] for r in results], axis=0).reshape(1, 128, 512)
    return tuple(np.ascontiguousarray(a, dtype=np.float32) for a in (yp, ys, mk, mv, cap, cbp, hp, cas, cbs, hs))


def kernel(**inputs):
    nc = build()
    maps = make_in_maps(inputs)
    res = run_bass_kernel_spmd(nc, maps, core_ids=list(range(8)))
    return assemble(res.results)
```

```python
import numpy as np
import concourse.bass as bass
import concourse.mybir as mybir
from concourse.bass_utils import run_bass_kernel_spmd
from contextlib import ExitStack

F32 = mybir.dt.float32
BF16 = mybir.dt.bfloat16
AF = mybir.ActivationFunctionType
ALU = mybir.AluOpType
AX = mybir.AxisListType

D = 1024
FF = 2816
T = 2048
PT = 512
NPASS = T // PT
NSMP = 16
NM = 256
W_ALL = PT + NSMP
NSLOT = 6
SLOT_ELEMS = 22 * 128
LOOKAHEAD = NSLOT - 1
EPS = 1e-6
NBANK = 5
K_PRE = 20
K_POST = 28
C0 = 0.7978845608028654
C1 = 0.044715
DEBUG = {}
PHASE = ['init']
CURP = ['-']
FLAT = True


class NT:
    def __init__(self, i, c0, w, sample=False):
        self.i = i
        self.c0 = c0
        self.w = w
        self.sample = sample
        self.cols = slice(c0, c0 + w)


class Sched:
    ENG = ('pe', 'act', 'dve', 'pool', 'sp')

    def __init__(self):
        self.streams = {e: [] for e in self.ENG}
        self.count = {}
        self.state = {}
        self.waited = {e: {} for e in self.ENG}

    def _deps(self, reads, writes):
        deps = {}

        def add(ev):
            if ev is None:
                return
            s, v = ev
            if deps.get(s, 0) < v:
                deps[s] = v
        for k in reads:
            st = self.state.get(k)
            if st:
                add(st[0])
        for k in writes:
            st = self.state.get(k)
            if st:
                add(st[0])
                for s, v in st[1].items():
                    add((s, v))
        return deps

    def op(self, eng, fn, reads=(), writes=(), dma_sem=None):
        psr = [k for k in reads if isinstance(k, tuple) and k[0] == 'ps']
        if psr:
            reads = [k for k in reads if not (isinstance(k, tuple) and k[0] == 'ps')]
            writes = list(writes) + psr
        deps = self._deps(reads, writes)
        waits = []
        for s, v in deps.items():
            if s == 'pe' and eng == 'pe':
                continue
            if self.waited[eng].get(s, 0) >= v:
                continue
            self.waited[eng][s] = v
            waits.append((s, v))
        if dma_sem is not None:
            self.count[dma_sem] = self.count.get(dma_sem, 0) + 16
            ev = (dma_sem, self.count[dma_sem])
            inc = (dma_sem, 16)
        else:
            self.count[eng] = self.count.get(eng, 0) + 1
            ev = (eng, self.count[eng])
            inc = (eng, 1)
        for k in reads:
            st = self.state.setdefault(k, [None, {}])
            if st[1].get(ev[0], 0) < ev[1]:
                st[1][ev[0]] = ev[1]
        for k in writes:
            self.state[k] = [ev, {}]
        self.streams[eng].append((waits, fn, inc, PHASE[0]))
        return ev


def build(stop=None):
    nc = bass.Bass("TRN2", target_bir_lowering=False)
    S = Sched()
    op = S.op

    def din(name, shape):
        return nc.dram_tensor(name, list(shape), F32, kind="ExternalInput").ap()

    def dout(name, shape):
        return nc.dram_tensor(name, list(shape), F32, kind="ExternalOutput").ap()

    x_d = din("x", [T, D])
    xs_d = din("xs", [NSMP, D])
    mem_d = din("mem", [NM, D])
    ck_d = din("ck", [NSMP, NM, D])
    cv_d = din("cv", [NSMP, NM, D])
    sca_d = din("sca", [NSMP, 2 * 512])
    scb_d = din("scb", [NSMP, 3 * 512])
    slru_d = din("slru", [NSMP, 512])
    vec1024 = ["g_ffn1_pre", "g_ffn1_post", "g_mix_pre", "g_mix_post", "g_xattn_pre", "g_xattn_post",
               "g_mem", "g_ffn2_pre", "g_ffn2_post"]
    vec_d = {n: din(n, [1, D]) for n in vec1024}
    caw_d = din("conv_a_w", [3, 512])
    cbw_d = din("conv_b_w", [4, 512])
    v512 = ["conv_b_b", "lru_ba", "lru_bx", "lru_lam"]
    v512_d = {n: din(n, [1, 512]) for n in v512}
    lwa_d = din("lru_wa", [8, 64, 64])
    lwx_d = din("lru_wx", [8, 64, 64])
    Wd = {}
    for n, shp in [("ffn1_wg", (D, FF)), ("ffn1_wu", (D, FF)), ("ffn1_wd", (FF, D)), ("w_in", (D, 2560)),
                   ("w_out", (D, D)), ("xattn_wq", (D, D)), ("xattn_wk", (D, D)), ("xattn_wv", (D, D)),
                   ("xattn_wo", (D, D)), ("ffn2_wg", (D, FF)), ("ffn2_wu", (D, FF)), ("ffn2_wd", (FF, D))]:
        Wd[n] = din(n, shp)

    y_d = dout("y", [T, D])
    ys_d = dout("ys", [NSMP, D])
    mk_d = dout("mk", [NM, D])
    mv_d = dout("mv", [NM, D])
    cap_d = dout("cap", [2, 512])
    cbp_d = dout("cbp", [3, 512])
    hp_d = dout("hp", [1, 512])
    cas_d = dout("cas", [NSMP, 2 * 512])
    cbs_d = dout("cbs", [NSMP, 3 * 512])
    hs_d = dout("hs", [NSMP, 512])

    es = ExitStack()

    def sb(name, shape, dt=F32):
        if not DEBUG.get('flat', FLAT):
            return es.enter_context(nc.sbuf_tensor(name, list(shape), dt))
        esz = 2 if dt == BF16 else 4
        n = 1
        for d_ in shape[1:]:
            n *= d_
        nbytes = n * esz
        assert nbytes % 4 == 0
        t = es.enter_context(nc.sbuf_tensor(name, [shape[0], nbytes // 4], F32))
        ap = t[:, :]
        if dt != F32:
            ap = ap.bitcast(dt)
        if len(shape) == 3:
            ap = ap.rearrange("p (a b) -> p a b", a=shape[1])
        elif len(shape) == 4:
            ap = ap.rearrange("p (a b c) -> p a b c", a=shape[1], b=shape[2])
        return ap

    ident = sb("ident", [128, 128])
    ones32 = sb("ones32", [128, 128])
    ones_bf = sb("ones_bf", [128, 128], BF16)
    epsb = sb("epsb", [128, 8])
    wrm = sb("wrm", [128, 512], BF16)
    xpre = sb("xpre", [128, 2, 1024])
    xT = sb("xT", [128, 8, W_ALL])
    hT = sb("hT", [128, 8, W_ALL], BF16)
    big = sb("big", [128, 22, W_ALL], BF16)
    ybuf = sb("ybuf", [128, 8, 1, 516])
    ybs = sb("ybs", [128, 8, NSMP])
    sq = sb("sq", [128, 4, 512], BF16)
    ring = sb("wslab", [128, NSLOT, SLOT_ELEMS], BF16)
    xin = sb("xin", [128, 2, 1024])
    NV = len(vec1024)
    cst = sb("cst", [128, 8, 32])
    cst5 = sb("cst5", [128, 4, 32])
    NR = 3
    rs = sb("rs", [128, NR, 512])
    NTMP = 8
    tmp = sb("tmp", [128, NTMP, 512])
    cb16 = sb("cb16", [128, 2, 512], BF16)
    bd = sb("bd", [128, 8, 128], BF16)
    carryA = sb("carryA", [128, 4, 2])
    carryB = sb("carryB", [128, 4, 3])
    hcar = sb("hcar", [128, 4, 1])
    KT = sb("KT", [128, 8, NM], BF16)
    Vn = sb("Vn", [128, 2, D], BF16)
    pT = sb("pT", [128, 2, 2, 512], BF16)
    rden = sb("rden", [128, 1, 512])
    Rs = sb("Rs", [128, 24, NSMP])
    SA = sb("SA", [128, 4, 2, NSMP])
    SB_ = sb("SB_", [128, 4, 3, NSMP])
    SH = sb("SH", [128, 4, NSMP])
    qs32 = sb("qs32", [128, 8, NSMP])
    qtok = sb("qtok", [NSMP, 1024], BF16)
    sel = sb("sel", [NSMP, NSMP, 128], BF16)
    Ks = sb("Ks", [128, 2, 2, D], BF16)
    Vs = sb("Vs", [128, 2, 2, D], BF16)
    sc = sb("sc", [128, 8])
    e16 = sb("e16", [128, 2, 8], BF16)
    rdens = sb("rdens", [128, NSMP, 4])
    c_lru = sb("c_lru", [128, 4, 4])
    ps = es.enter_context(nc.psum_tensor("ps", [128, 8, 512], F32))

    vstage = xin[0:32, 0, :]
    v5stage = xin[0:32, 1, 0:512]
    prod = tmp[:, 0:2, :].rearrange("p a b -> p (a b)")
    PRODK = [('tmp', 0), ('tmp', 1)]

    def kt32(c):
        return ybuf[:, c, 0, 0:NM], ('ybuf', c, 0)

    def memT(c):
        return hT[:, c, 0:NM], ('hT', c, 0)
    VI = {n: i for i, n in enumerate(vec1024)}
    GPOST = {"g_ffn1_post": 16, "g_ffn2_post": 17, "g_mix_post": VI["g_mix_post"], "g_xattn_post": VI["g_xattn_post"]}
    V5 = {"caw0": 0, "caw1": 1, "caw2": 2, "cbw0": 3, "cbw1": 4, "cbw2": 5, "cbw3": 6,
          "conv_b_b": 7, "lru_ba": 8, "lru_bx": 9, "lru_lam": 10}

    rot = {}

    def nxt(name, n):
        v = rot.get(name, 0)
        rot[name] = (v + 1) % n
        return v

    def bank():
        return nxt('bank', NBANK)

    dma_sems = set()

    def dma(q, out, in_, sem, reads=(), writes=()):
        dma_sems.add(sem)
        return op(q, lambda e, o=out, i=in_: e.dma_start(out=o, in_=i), reads=reads, writes=writes, dma_sem=sem)

    def act(out, in_, func, reads, writes, **kw):
        return op('act', lambda e: e.activation(out=out, in_=in_, func=func, **kw), reads=reads, writes=writes)

    def tt(eng, out, in0, in1, o, reads, writes):
        return op(eng, lambda e: e.tensor_tensor(out=out, in0=in0, in1=in1, op=o), reads=reads, writes=writes)

    def ts(eng, out, in0, s1, s2, o0, o1, reads, writes):
        if o1 is None:
            return op(eng, lambda e: e.tensor_scalar(out=out, in0=in0, scalar1=s1, scalar2=None, op0=o0),
                      reads=reads, writes=writes)
        return op(eng, lambda e: e.tensor_scalar(out=out, in0=in0, scalar1=s1, scalar2=s2, op0=o0, op1=o1),
                  reads=reads, writes=writes)

    def stt(out, in0, scalar, in1, o0, o1, reads, writes):
        return op('dve', lambda e: e.scalar_tensor_tensor(out=out, in0=in0, scalar=scalar, in1=in1, op0=o0, op1=o1),
                  reads=reads, writes=writes)

    def copy(eng, out, in_, reads, writes):
        if eng == 'act':
            return act(out, in_, AF.Copy, reads, writes)
        return op(eng, lambda e: e.tensor_copy(out=out, in_=in_), reads=reads, writes=writes)

    def memset(eng, ap, val, writes):
        return op(eng, lambda e: e.memset(ap, val), writes=writes)

    def transpose(out, in_, reads, writes, np_in=128):
        return op('pe', lambda e: e.transpose(out, in_, ident[:np_in, :np_in]), reads=list(reads) + ['ident'],
                  writes=writes)

    slabs = []
    slab_pos = [0]
    slab_issued = [0]

    def issue_upto(j):
        while slab_issued[0] <= j and slab_issued[0] < len(slabs):
            i = slab_issued[0]
            wname, col0, nk = slabs[i]
            slot = i % NSLOT
            src = Wd[wname].rearrange("(kc p) n -> p kc n", p=128)[:, :, col0:col0 + 128]
            if DEBUG.get('srcx'):
                src = x_d[0:1024, :].rearrange("(kc p) n -> p kc n", p=128)[:, :, col0:col0 + 128]
            dst = ring[:, slot, 0:nk * 128].rearrange("p (kc n) -> p kc n", n=128)
            dma('pool', dst, src, ('ring' if DEBUG.get('onesem') else 'ring%d' % slot), writes=[('ring', slot)])
            if DEBUG.get('serial'):
                op('pool', lambda e: e.memset(sc[:, 0:1], 0.0), reads=[('ring', slot)], writes=['scdummy'])
            slab_issued[0] += 1

    def next_slab(wname, col0, nk, ahead=LOOKAHEAD):
        i = slab_pos[0]
        assert slabs[i] == (wname, col0, nk), (i, slabs[i], wname, col0, nk)
        issue_upto(i + ahead)
        slab_pos[0] += 1
        return i % NSLOT

    def linear(jobs, nk, in_fn, nts, epilogue):
        PHASE[0] = "p%s:lin:%s:%d" % (CURP[0], jobs[0][0][0], jobs[0][0][1])
        for ji, job in enumerate(jobs):
            nj = len(job)
            slots = [next_slab(wn, cc * 128, nk, LOOKAHEAD - (nj - 1) - j) for j, (wn, cc) in enumerate(job)]
            if DEBUG.get('lin') == 'dma':
                continue
            for nt in nts:
                banks = []
                for slot in slots:
                    b = bank()
                    banks.append(b)
                    ins = [in_fn(kc, nt) for kc in range(nk)]

                    def fn(e, slot=slot, b=b, ins=ins, w=nt.w):
                        last = None
                        for kc in range(nk):
                            last = e.matmul(ps[:, b, :w], lhsT=ring[:, slot, kc * 128:(kc + 1) * 128],
                                            rhs=ins[kc][0], start=(kc == 0), stop=(kc == nk - 1))
                        return last
                    op('pe', fn, reads=[('ring', slot)] + [k for _, k in ins], writes=[('ps', b)])
                if DEBUG.get('lin') != 'mm':
                    epilogue(ji, nt, banks)

    def rstd_from_psum(b, w):
        ri = nxt('rs', NR)
        act(rs[:, ri, :w], ps[:, b, :w], AF.Ln, [('ps', b), 'consts'], [('rs', ri)], scale=1.0 / D, bias=epsb[:, 0:1])
        act(rs[:, ri, :w], rs[:, ri, :w], AF.Exp, [('rs', ri)], [('rs', ri)], scale=-0.5)
        return ri

    def warm(k):
        if k <= 0 or DEBUG.get('nowarm'):
            return

        def fn(e):
            last = None
            for _ in range(k):
                last = e.matmul(ps[:, 5, :], lhsT=ones_bf[:], rhs=wrm[:, :], start=True, stop=True)
            return last
        op('pe', fn, reads=['consts'], writes=[('ps', 5)])

    def norm_stats(src_fn, w):
        b = bank()
        for c in range(8):
            ap, key = src_fn(c)
            si = nxt('sq', 4)
            act(sq[:, si, :w], ap, AF.Square, [key], [('sq', si)])
            op('pe', lambda e, si=si, c=c: e.matmul(ps[:, b, :w], lhsT=ones_bf[:], rhs=sq[:, si, :w],
                                                   start=(c == 0), stop=(c == 7)),
               reads=[('sq', si), 'consts'], writes=[('ps', b)])
        return rstd_from_psum(b, w)

    def xkey(c, nt):
        return ('xT', c, nt.i)

    def prenorm(nt, gname):
        PHASE[0] = 'p%s:pre:%s' % (CURP[0], gname)
        gi = VI[gname]
        ri = norm_stats(lambda c: (xT[:, c, nt.cols], xkey(c, nt)), nt.w)
        if not nt.sample:
            warm(K_PRE)
        for c in range(8):
            stt(hT[:, c, nt.cols], xT[:, c, nt.cols], cst[:, c, gi:gi + 1], rs[:, ri, :nt.w], ALU.mult, ALU.mult,
                [xkey(c, nt), ('rs', ri), 'cst'], [('hT', c, nt.i)])

    def yb_ap(c, nt):
        if nt.sample:
            return ybs[:, c, :], ('ybs', c)
        return ybuf[:, c, nt.i, 4:4 + nt.w], ('ybuf', c, nt.i)

    def stat_bank(nt):
        return 6 if nt.sample else 7

    def y_epilogue(c, nt, b, gname):
        gi = GPOST[gname]
        ap, key = yb_ap(c, nt)
        w = nt.w
        sbk = stat_bank(nt)
        si = nxt('sq', 4)
        act(sq[:, si, :w], ps[:, b, :w], AF.Square, [('ps', b)], [('sq', si)])
        act(ap, ps[:, b, :w], AF.Copy, [('ps', b), 'cst'], [key], scale=cst[:, c, gi:gi + 1])
        op('pe', lambda e: e.matmul(ps[:, sbk, :w], lhsT=ones_bf[:], rhs=sq[:, si, :w], start=(c == 0), stop=(c == 7)),
           reads=[('sq', si), 'consts'], writes=[('ps', sbk)])

    def postnorm(nt, gname, coef):
        PHASE[0] = 'p%s:post:%s' % (CURP[0], gname)
        w = nt.w
        if not nt.sample:
            warm(K_POST)
        ri = rstd_from_psum(stat_bank(nt), w)
        for c in range(8):
            ap, key = yb_ap(c, nt)
            ti = nxt('tmp', NTMP)
            tt('dve', tmp[:, ti, :w], ap, rs[:, ri, :w], ALU.mult, [key, ('rs', ri)], [('tmp', ti)])
            tt('dve', xT[:, c, nt.cols], xT[:, c, nt.cols], tmp[:, ti, :w],
               ALU.add, [('tmp', ti), xkey(c, nt)], [xkey(c, nt)])

    def hT_in(kc, nt):
        return hT[:, kc, nt.cols], ('hT', kc, nt.i)

    def ffn(nts, pre, wg, wu, wd, gpre, gpost):
        for nt in nts:
            prenorm(nt, gpre)

        def ep1(m, nt, banks):
            bg, bu = banks
            ti = nxt('tmp', NTMP)
            act(tmp[:, ti, :nt.w], ps[:, bg, :nt.w], AF.Silu, [('ps', bg)], [('tmp', ti)])
            tt('dve', big[:, m, nt.cols], tmp[:, ti, :nt.w], ps[:, bu, :nt.w], ALU.mult,
               [('tmp', ti), ('ps', bu)], [('big', m, nt.i)])
        linear([[(wg, m), (wu, m)] for m in range(22)], 8, hT_in, nts, ep1)

        def ep2(c, nt, banks):
            y_epilogue(c, nt, banks[0], gpost)
        linear([[(wd, c)] for c in range(8)], 22, lambda kc, nt: (big[:, kc, nt.cols], ('big', kc, nt.i)), nts, ep2)
        for nt in nts:
            postnorm(nt, gpost, 0.5)

    def ffn_slabs(wg, wu, wd):
        out = []
        for m in range(22):
            out += [(wg, m * 128, 8), (wu, m * 128, 8)]
        out += [(wd, c * 128, 22) for c in range(8)]
        return out

    def R(i, nt):
        if nt.sample:
            return None
        if i < 8:
            return ybuf[:, i, nt.i, :], [('ybuf', i, nt.i)]
        ch = 8 + 2 * (i - 8)
        return (big[:, ch:ch + 2, :].rearrange("p a b -> p (a b)").bitcast(F32)[:, 0:516],
                [('big', ch, 0), ('big', ch + 1, 0)])

    def mix(nts, p):
        for nt in nts:
            prenorm(nt, "g_mix_pre")
        ptiles = [nt for nt in nts if not nt.sample]
        stiles = [nt for nt in nts if nt.sample]

        def c5(c, name):
            i = V5[name]
            return cst5[:, c, i:i + 1]

        def ep_xb(c, nt, banks):
            b = banks[0]
            if nt.sample:
                copy('act', Rs[:, 4 + c, :], ps[:, b, :NSMP], [('ps', b)], [('Rs', 4 + c)])
                return
            r, keys = R(4 + c, nt)
            copy('dve', r[:, 1:4], carryB[:, c, :], [('carryB', c)], keys)
            copy('act', r[:, 4:516], ps[:, b, :512], [('ps', b)], keys)
            copy('dve', carryB[:, c, :], r[:, 513:516], keys, [('carryB', c)])
        linear([[("w_in", 12 + c)] for c in range(4)], 8, hT_in, nts, ep_xb)

        def conv_b(c, nt):
            if nt.sample:
                o = Rs[:, 8 + c, :]
                ts('dve', o, SB_[:, c, 0, :], c5(c, "cbw0"), c5(c, "conv_b_b"), ALU.mult, ALU.add,
                   ['SB_', 'cst5'], [('Rs', 8 + c)])
                for k in (1, 2):
                    stt(o, SB_[:, c, k, :], c5(c, "cbw%d" % k), o, ALU.mult, ALU.add, ['SB_', 'cst5', ('Rs', 8 + c)],
                        [('Rs', 8 + c)])
                stt(o, Rs[:, 4 + c, :], c5(c, "cbw3"), o, ALU.mult, ALU.add, [('Rs', 4 + c), 'cst5', ('Rs', 8 + c)],
                    [('Rs', 8 + c)])
                return
            x_, xk = R(4 + c, nt)
            cbr, ck = R(10 + c, nt)
            o = cbr[:, 4:516]
            ts('dve', o, x_[:, 1:513], c5(c, "cbw0"), c5(c, "conv_b_b"), ALU.mult, ALU.add, xk + ['cst5'], ck)
            for k in (1, 2, 3):
                stt(o, x_[:, 1 + k:513 + k], c5(c, "cbw%d" % k), o, ALU.mult, ALU.add, xk + ck + ['cst5'], ck)

        for c in range(4):
            for nt in nts:
                conv_b(c, nt)

        def ep_A(which):
            def ep(c_rel, nt, banks):
                c = ep.c0 + c_rel // 3
                role = c_rel % 3
                b = banks[0]
                w = nt.w
                if nt.sample:
                    if role == 0:
                        copy('act', Rs[:, c, :], ps[:, b, :w], [('ps', b)], [('Rs', c)])
                    elif role == 1:
                        tt('dve', Rs[:, c, :], Rs[:, c, :], ps[:, b, :w], ALU.mult, [('Rs', c), ('ps', b)], [('Rs', c)])
                        o = Rs[:, 12 + c, :]
                        ts('dve', o, SA[:, c, 0, :], c5(c, "caw0"), None, ALU.mult, None, ['SA', 'cst5'],
                           [('Rs', 12 + c)])
                        stt(o, SA[:, c, 1, :], c5(c, "caw1"), o, ALU.mult, ALU.add, ['SA', 'cst5', ('Rs', 12 + c)],
                            [('Rs', 12 + c)])
                        stt(o, Rs[:, c, :], c5(c, "caw2"), o, ALU.mult, ALU.add, [('Rs', c), 'cst5', ('Rs', 12 + c)],
                            [('Rs', 12 + c)])
                    else:
                        tt('dve', big[:, c, nt.cols], Rs[:, 12 + c, :], ps[:, b, :w], ALU.mult,
                           [('Rs', 12 + c), ('ps', b)], [('big', c, nt.i)])
                    return
                v_, vk = R(c, nt)
                ca_, cak = R(8 + (c % 2), nt)
                if role == 0:
                    copy('dve', v_[:, 2:4], carryA[:, c, :], [('carryA', c)], vk)
                    copy('act', v_[:, 4:516], ps[:, b, :512], [('ps', b)], vk)
                elif role == 1:
                    tt('dve', v_[:, 4:516], v_[:, 4:516], ps[:, b, :512], ALU.mult, vk + [('ps', b)], vk)
                    copy('dve', carryA[:, c, :], v_[:, 514:516], vk, [('carryA', c)])
                    o = ca_[:, 4:516]
                    ts('dve', o, v_[:, 2:514], c5(c, "caw0"), None, ALU.mult, None, vk + ['cst5'], cak)
                    stt(o, v_[:, 3:515], c5(c, "caw1"), o, ALU.mult, ALU.add, vk + cak + ['cst5'], cak)
                    stt(o, v_[:, 4:516], c5(c, "caw2"), o, ALU.mult, ALU.add, vk + cak + ['cst5'], cak)
                else:
                    tt('dve', big[:, c, nt.cols], ca_[:, 4:516], ps[:, b, :512], ALU.mult, cak + [('ps', b)],
                       [('big', c, nt.i)])
            ep.c0 = which
            return ep

        def A_jobs(c0, n):
            jobs = []
            for c in range(c0, c0 + n):
                jobs += [[("w_in", 8 + c)], [("w_in", 4 + c)], [("w_in", c)]]
            return jobs
        linear(A_jobs(0, 2), 8, hT_in, nts, ep_A(0))

        def lru(c, nt):
            PHASE[0] = 'p%s:lru' % CURP[0]
            w = nt.w
            if nt.sample:
                cbv = Rs[:, 8 + c, :]
                cbk = [('Rs', 8 + c)]
            else:
                cbr, cbk = R(10 + c, nt)
                cbv = cbr[:, 4:516]
            ci = nxt('cb16', 2)
            copy('act', cb16[:, ci, :w], cbv, cbk, [('cb16', ci)])
            ba, bx = bank(), bank()
            op('pe', lambda e: e.matmul(ps[:, ba, :w], lhsT=bd[:, c, :], rhs=cb16[:, ci, :w], start=True, stop=True),
               reads=[('cb16', ci), 'bd'], writes=[('ps', ba)])
            op('pe', lambda e: e.matmul(ps[:, bx, :w], lhsT=bd[:, 4 + c, :], rhs=cb16[:, ci, :w], start=True, stop=True),
               reads=[('cb16', ci), 'bd'], writes=[('ps', bx)])
            t_a, t_x, t_aa, t_m = [nxt('tmp', NTMP) for _ in range(4)]
            act(tmp[:, t_a, :w], ps[:, ba, :w], AF.Tanh, [('ps', ba), 'c_lru'], [('tmp', t_a)],
                scale=0.5, bias=c_lru[:, c, 2:3])
            act(tmp[:, t_x, :w], ps[:, bx, :w], AF.Tanh, [('ps', bx), 'c_lru'], [('tmp', t_x)],
                scale=0.5, bias=c_lru[:, c, 3:4])
            act(tmp[:, t_aa, :w], tmp[:, t_a, :w], AF.Exp, [('tmp', t_a), 'c_lru'], [('tmp', t_aa)],
                scale=c_lru[:, c, 0:1], bias=c_lru[:, c, 0:1])
            act(tmp[:, t_m, :w], tmp[:, t_a, :w], AF.Exp, [('tmp', t_a), 'c_lru'], [('tmp', t_m)],
                scale=c_lru[:, c, 1:2], bias=c_lru[:, c, 1:2])
            ts('dve', tmp[:, t_m, :w], tmp[:, t_m, :w], -1.0, 1.0, ALU.mult, ALU.add, [('tmp', t_m)], [('tmp', t_m)])
            act(tmp[:, t_m, :w], tmp[:, t_m, :w], AF.Sqrt, [('tmp', t_m)], [('tmp', t_m)])
            if (not nt.sample) and p == 0 and nt.i == 0:
                memset('dve', tmp[:, t_m, 0:1], 1.0, [('tmp', t_m)])
            stt(tmp[:, t_x, :w], tmp[:, t_x, :w], 1.0, cbv, ALU.add, ALU.mult, [('tmp', t_x)] + cbk, [('tmp', t_x)])
            stt(tmp[:, t_x, :w], tmp[:, t_x, :w], 0.5, tmp[:, t_m, :w], ALU.mult, ALU.mult,
                [('tmp', t_x), ('tmp', t_m)], [('tmp', t_x)])
            if nt.sample:
                hb = Rs[:, 16 + c, :]
                tt('dve', hb, tmp[:, t_aa, :w], SH[:, c, :], ALU.mult, [('tmp', t_aa), 'SH'], [('Rs', 16 + c)])
                tt('dve', hb, hb, tmp[:, t_x, :w], ALU.add, [('Rs', 16 + c), ('tmp', t_x)], [('Rs', 16 + c)])
            else:
                hbr, hk = R(10 + c, nt)
                op('dve', lambda e: e.tensor_tensor_scan(out=hbr[:, 4:516], data0=tmp[:, t_aa, :w],
                                                         data1=tmp[:, t_x, :w], initial=hcar[:, c, :],
                                                         op0=ALU.mult, op1=ALU.add),
                   reads=[('tmp', t_aa), ('tmp', t_x), ('hcar', c)], writes=hk)
                copy('dve', hcar[:, c, :], hbr[:, 515:516], hk, [('hcar', c)])

        for c in range(4):
            for nt in nts:
                lru(c, nt)

        linear(A_jobs(2, 2), 8, hT_in, nts, ep_A(2))

        def ep_gg(c, nt, banks):
            b = banks[0]
            w = nt.w
            g = ps[:, b, :w]
            if nt.sample:
                hv, hk = Rs[:, 16 + c, :], [('Rs', 16 + c)]
            else:
                hbr, hk = R(10 + c, nt)
                hv = hbr[:, 4:516]
            t1, t2 = nxt('tmp', NTMP), nxt('tmp', NTMP)
            act(tmp[:, t1, :w], g, AF.Square, [('ps', b)], [('tmp', t1)], scale=float(np.sqrt(C1)))
            stt(tmp[:, t1, :w], tmp[:, t1, :w], 1.0, g, ALU.add, ALU.mult, [('tmp', t1), ('ps', b)], [('tmp', t1)])
            act(tmp[:, t2, :w], tmp[:, t1, :w], AF.Tanh, [('tmp', t1)], [('tmp', t2)], scale=C0)
            stt(tmp[:, t2, :w], tmp[:, t2, :w], 1.0, g, ALU.add, ALU.mult, [('tmp', t2), ('ps', b)], [('tmp', t2)])
            stt(big[:, 4 + c, nt.cols], tmp[:, t2, :w], 0.5, hv, ALU.mult, ALU.mult, [('tmp', t2)] + hk,
                [('big', 4 + c, nt.i)])
        linear([[("w_in", 16 + c)] for c in range(4)], 8, hT_in, nts, ep_gg)

        def ep2(c, nt, banks):
            y_epilogue(c, nt, banks[0], "g_mix_post")
        linear([[("w_out", c)] for c in range(8)], 8, lambda kc, nt: (big[:, kc, nt.cols], ('big', kc, nt.i)), nts, ep2)
        for nt in nts:
            postnorm(nt, "g_mix_post", 1.0)

    def mix_slabs():
        out = [("w_in", (12 + c) * 128, 8) for c in range(4)]
        for c in range(4):
            out += [("w_in", (8 + c) * 128, 8), ("w_in", (4 + c) * 128, 8), ("w_in", c * 128, 8)]
        out += [("w_in", (16 + c) * 128, 8) for c in range(4)]
        out += [("w_out", c * 128, 8) for c in range(8)]
        return out

    def prompt_attn(nt):
        PHASE[0] = 'p%s:pattn' % CURP[0]
        w = nt.w
        for h in range(4):
            pi = nxt('pT', 2)
            for mc in range(2):
                b = bank()

                def fn(e, b=b, mc=mc, h=h):
                    e.matmul(ps[:, b, :w], lhsT=KT[:, 2 * h, mc * 128:(mc + 1) * 128], rhs=big[:, 2 * h, nt.cols],
                             start=True, stop=False)
                    return e.matmul(ps[:, b, :w], lhsT=KT[:, 2 * h + 1, mc * 128:(mc + 1) * 128],
                                    rhs=big[:, 2 * h + 1, nt.cols], start=False, stop=True)
                op('pe', fn, reads=['KT', ('big', 2 * h, nt.i), ('big', 2 * h + 1, nt.i)], writes=[('ps', b)])
                act(pT[:, pi, mc, :w], ps[:, b, :w], AF.Exp, [('ps', b)], [('pT', pi, mc)], scale=1.0 / 16.0)
            bden = bank()

            def fnd(e, bden=bden, pi=pi):
                e.matmul(ps[:, bden, :w], lhsT=ones_bf[:], rhs=pT[:, pi, 0, :w], start=True, stop=False)
                return e.matmul(ps[:, bden, :w], lhsT=ones_bf[:], rhs=pT[:, pi, 1, :w], start=False, stop=True)
            op('pe', fnd, reads=[('pT', pi, 0), ('pT', pi, 1), 'consts'], writes=[('ps', bden)])
            act(rden[:, 0, :w], ps[:, bden, :w], AF.Ln, [('ps', bden)], ['rden'])
            act(rden[:, 0, :w], rden[:, 0, :w], AF.Exp, ['rden'], ['rden'], scale=-1.0)
            for ee in range(2):
                bo = bank()
                cc = 2 * h + ee

                def fno(e, bo=bo, cc=cc, pi=pi):
                    e.matmul(ps[:, bo, :w], lhsT=Vn[:, 0, cc * 128:(cc + 1) * 128], rhs=pT[:, pi, 0, :w],
                             start=True, stop=False)
                    return e.matmul(ps[:, bo, :w], lhsT=Vn[:, 1, cc * 128:(cc + 1) * 128], rhs=pT[:, pi, 1, :w],
                                    start=False, stop=True)
                op('pe', fno, reads=['Vn', ('pT', pi, 0), ('pT', pi, 1)], writes=[('ps', bo)])
                tt('dve', big[:, 8 + cc, nt.cols], ps[:, bo, :w], rden[:, 0, :w], ALU.mult,
                   [('ps', bo), 'rden'], [('big', 8 + cc, nt.i)])

    def xattn(nts, p):
        for nt in nts:
            prenorm(nt, "g_xattn_pre")

        def ep_q(c, nt, banks):
            b = banks[0]
            copy('act', big[:, c, nt.cols], ps[:, b, :nt.w], [('ps', b)], [('big', c, nt.i)])
            if nt.sample:
                copy('dve', qs32[:, c, :], ps[:, b, :nt.w], [('ps', b)], [('qs32', c)])
        linear([[("xattn_wq", c)] for c in range(8)], 8, hT_in, nts, ep_q)

        for nt in nts:
            if nt.sample:
                sample_attn(nt)
            else:
                prompt_attn(nt)

        def ep2(c, nt, banks):
            y_epilogue(c, nt, banks[0], "g_xattn_post")
        linear([[("xattn_wo", c)] for c in range(8)], 8,
               lambda kc, nt: (big[:, 8 + kc, nt.cols], ('big', 8 + kc, nt.i)), nts, ep2)
        for nt in nts:
            postnorm(nt, "g_xattn_post", 1.0)

    def xattn_slabs():
        return [("xattn_wq", c * 128, 8) for c in range(8)] + [("xattn_wo", c * 128, 8) for c in range(8)]

    def sample_attn(nt):
        PHASE[0] = 'p%s:sattn' % CURP[0]
        for half in range(2):
            b = bank()
            for j in range(4):
                c = half * 4 + j
                transpose(ps[:NSMP, b, j * 128:(j + 1) * 128], qs32[:, c, :], [('qs32', c)], [('ps', b)])
            copy('act', qtok[:, half * 512:(half + 1) * 512], ps[:NSMP, b, :], [('ps', b)], [('qtok', half)])
        BO = 7
        def sel_mm(s):
            b0 = 0 if s % 2 == 0 else 2
            b1 = b0 + 1
            for hh, bb in ((0, b0), (1, b1)):
                op('pe', lambda e, hh=hh, bb=bb, s=s: e.matmul(ps[:, bb, :], lhsT=sel[:, s, :],
                                                              rhs=qtok[:, hh * 512:(hh + 1) * 512],
                                                              start=True, stop=True),
                   reads=[('qtok', hh), 'sel'], writes=[('ps', bb)])
            return b0, b1

        qb = {0: sel_mm(0)}
        for s in range(NSMP):
            ri = s % 2
            src_k = ck_d[s].rearrange("(mc p) d -> p mc d", p=128)
            src_v = cv_d[s].rearrange("(mc p) d -> p mc d", p=128)
            dma('pool', Ks[:, ri, :, :], src_k, 'ks%d' % ri, writes=[('Ks', ri)])
            dma('pool', Vs[:, ri, :, :], src_v, 'vs%d' % ri, writes=[('Vs', ri)])
            if s + 1 < NSMP:
                qb[s + 1] = sel_mm(s + 1)
            b0, b1 = qb.pop(s)
            qbc = ps[:, b0:b0 + 2, :].rearrange("p a b -> p (a b)")
            for mc in range(2):
                tt('dve', prod, Ks[:, ri, mc, :], qbc, ALU.mult, [('Ks', ri), ('ps', b0), ('ps', b1)], PRODK)
                op('dve', lambda e, mc=mc: e.tensor_reduce(out=sc[:, mc * 4:(mc + 1) * 4],
                                                          in_=prod.rearrange("p (h d) -> p h d", d=256),
                                                          axis=AX.X, op=ALU.add),
                   reads=PRODK, writes=[('sc', mc)])
            ei = nxt('e16', 2)
            act(e16[:, ei, :], sc[:, :], AF.Exp, [('sc', 0), ('sc', 1)], [('e16', ei)], scale=1.0 / 16.0)
            bden = 4
            op('pe', lambda e, bden=bden, ei=ei: e.matmul(ps[:, bden, 0:8], lhsT=ones_bf[:], rhs=e16[:, ei, :],
                                                         start=True, stop=True),
               reads=[('e16', ei), 'consts'], writes=[('ps', bden)])
            op('dve', lambda e, bden=bden, s=s: e.tensor_reduce(
                out=rdens[:, s, :], in_=ps[:, bden, 0:8].rearrange("p (mc h) -> p h mc", mc=2),
                axis=AX.X, op=ALU.add), reads=[('ps', bden)], writes=[('rdens', s)])
            op('dve', lambda e, s=s: e.reciprocal(out=rdens[:, s, :], in_=rdens[:, s, :]),
               reads=[('rdens', s)], writes=[('rdens', s)])

            def fpv(e, s=s, ri=ri, ei=ei):
                last = None
                for c in range(8):
                    h = c // 2
                    for mc in range(2):
                        last = e.matmul(ps[:, BO, c * NSMP + s:c * NSMP + s + 1],
                                        lhsT=Vs[:, ri, mc, c * 128:(c + 1) * 128],
                                        rhs=e16[:, ei, mc * 4 + h:mc * 4 + h + 1], start=(mc == 0), stop=(mc == 1))
                return last
            op('pe', fpv, reads=[('Vs', ri), ('e16', ei)], writes=[('ps', BO)])
        for c in range(8):
            h = c // 2
            tt('dve', big[:, 8 + c, nt.cols], ps[:, BO, c * NSMP:(c + 1) * NSMP], rdens[:, :, h], ALU.mult,
               [('ps', BO)] + [('rdens', s) for s in range(NSMP)], [('big', 8 + c, nt.i)])

    XK0, XK1 = ('xin', 0), ('xin', 1)

    def setup():
        memset('dve', ones32[:], 1.0, ['ones32'])
        op('pool', lambda e: e.affine_select(out=ident[:], in_=ones32[:], pattern=[[1, 128]],
                                             compare_op=ALU.is_equal, fill=0.0, base=0, channel_multiplier=-1),
           reads=['ones32'], writes=['ident'])
        memset('dve', ones_bf[:], 1.0, ['consts'])
        memset('dve', epsb[:], EPS, ['consts'])
        memset('dve', wrm[:], 0.37, ['consts'])
        memset('dve', carryA[:], 0.0, [('carryA', c) for c in range(4)])
        memset('dve', carryB[:], 0.0, [('carryB', c) for c in range(4)])
        memset('dve', hcar[:], 0.0, [('hcar', c) for c in range(4)])
        memset('dve', bd[:], 0.0, ['bd'])
        memset('dve', xin[0:32, 0, :], 0.0, [XK0])
        memset('dve', xin[0:32, 1, :], 0.0, [XK1])
        for n in vec1024:
            dma('sp', vstage[VI[n]:VI[n] + 1, :], vec_d[n][0:1, :], 'cload', writes=[XK0])
        dma('sp', v5stage[0:3, :], caw_d[:, :], 'cload5', writes=[XK1])
        dma('sp', v5stage[3:7, :], cbw_d[:, :], 'cload5', writes=[XK1])
        for n in v512:
            dma('sp', v5stage[V5[n]:V5[n] + 1, :], v512_d[n][0:1, :], 'cload5', writes=[XK1])
        for g, wd_ in ((0, lwa_d), (1, lwx_d)):
            if DEBUG.get('nobd'):
                break
            for h in range(8):
                c, r = h // 2, h % 2
                dma('pool', bd[r * 64:(r + 1) * 64, g * 4 + c, r * 64:(r + 1) * 64], wd_[h], 'bdload',
                    writes=['bd'])
        for half in range(2):
            b = bank()
            for j in range(4):
                c = half * 4 + j
                transpose(ps[:, b, j * 32:(j + 1) * 32], vstage[:, c * 128:(c + 1) * 128], [XK0], [('ps', b)],
                          np_in=32)
            copy('dve', cst[:, half * 4:(half + 1) * 4, :], ps[:, b, 0:128].rearrange("p (j v) -> p j v", v=32),
                 [('ps', b)], ['cst'])
        b = bank()
        for c in range(4):
            transpose(ps[:, b, c * 32:(c + 1) * 32], v5stage[:, c * 128:(c + 1) * 128], [XK1], [('ps', b)],
                      np_in=32)
        copy('dve', cst5[:, :, :], ps[:, b, 0:128].rearrange("p (j v) -> p j v", v=32), [('ps', b)], ['cst5'])
        ts('dve', cst[:, :, 16], cst[:, :, VI["g_ffn1_post"]], 0.5, None, ALU.mult, None, ['cst'], ['cst'])
        ts('dve', cst[:, :, 17], cst[:, :, VI["g_ffn2_post"]], 0.5, None, ALU.mult, None, ['cst'], ['cst'])
        il, iba, ibx = V5["lru_lam"], V5["lru_ba"], V5["lru_bx"]
        K = ['c_lru']
        act(c_lru[:, :, 0], cst5[:, :, il], AF.Exp, ['cst5'], K, scale=-1.0)
        ts('dve', c_lru[:, :, 1], c_lru[:, :, 0], 1.0 / 3.0, -0.5, ALU.mult, ALU.add, K, K)
        tt('dve', c_lru[:, :, 1], c_lru[:, :, 1], c_lru[:, :, 0], ALU.mult, K, K)
        ts('dve', c_lru[:, :, 1], c_lru[:, :, 1], 1.0, None, ALU.add, None, K, K)
        tt('dve', c_lru[:, :, 1], c_lru[:, :, 1], c_lru[:, :, 0], ALU.mult, K, K)
        ts('dve', c_lru[:, :, 0], c_lru[:, :, 1], -4.0, None, ALU.mult, None, K, K)
        ts('dve', c_lru[:, :, 1], c_lru[:, :, 1], -8.0, None, ALU.mult, None, K, K)
        ts('dve', c_lru[:, :, 2], cst5[:, :, iba], 0.5, None, ALU.mult, None, ['cst5'] + K, K)
        ts('dve', c_lru[:, :, 3], cst5[:, :, ibx], 0.5, None, ALU.mult, None, ['cst5'] + K, K)
        copy('dve', sel[:, :, :], ident[:NSMP, :NSMP].unsqueeze(2).to_broadcast([NSMP, NSMP, 128]), ['ident'], ['sel'])

    def prefetch_x(p):
        for tcn in range(2):
            t0 = p * PT + tcn * 128
            dma('sp', xpre[:, tcn, :], x_d[t0:t0 + 128, :], 'xpre%d' % tcn, writes=[('xpre', tcn)])

    def load_x(p, nt):
        PHASE[0] = 'p%s:load' % p
        for tcn in range(PT // 128):
            t0 = p * PT + tcn * 128
            if tcn < 2:
                src, skey = xpre[:, tcn, :], ('xpre', tcn)
            else:
                xi = nxt('xin', 2)
                dma('sp', xin[:, xi, :], x_d[t0:t0 + 128, :], 'xin%d' % xi, writes=[('xin', xi)])
                src, skey = xin[:, xi, :], ('xin', xi)
            col = tcn * 128
            for half in range(2):
                b = bank()
                for j in range(4):
                    c = half * 4 + j
                    transpose(ps[:, b, j * 128:(j + 1) * 128], src[:, c * 128:(c + 1) * 128], [skey],
                              [('ps', b)])
                copy('act' if half == 0 else 'dve', xT[:, half * 4:(half + 1) * 4, col:col + 128],
                     ps[:, b, :].rearrange("p (j t) -> p j t", t=128), [('ps', b)],
                     [('xT', c, nt.i) for c in range(half * 4, half * 4 + 4)])

    def load_samples(nt):
        dma('sp', xin[0:NSMP, 0, :], xs_d[:, :], 'xin0', writes=[XK0])
        for half in range(2):
            b = bank()
            for j in range(4):
                c = half * 4 + j
                transpose(ps[:, b, j * NSMP:(j + 1) * NSMP], xin[0:NSMP, 0, c * 128:(c + 1) * 128], [XK0], [('ps', b)],
                          np_in=NSMP)
            copy('dve', xT[:, half * 4:(half + 1) * 4, nt.cols],
                 ps[:, b, 0:4 * NSMP].rearrange("p (j t) -> p j t", t=NSMP), [('ps', b)],
                 [('xT', c, nt.i) for c in range(half * 4, half * 4 + 4)])
        jobs = [(sca_d, 0, SA[:, :, 0, :], 'SA'), (sca_d, 1, SA[:, :, 1, :], 'SA'),
                (scb_d, 0, SB_[:, :, 0, :], 'SB_'), (scb_d, 1, SB_[:, :, 1, :], 'SB_'), (scb_d, 2, SB_[:, :, 2, :], 'SB_'),
                (slru_d, 0, SH[:, :, :], 'SH')]
        for (src, k, dst, key) in jobs:
            xi = nxt('xin', 2)
            dma('sp', xin[0:NSMP, xi, 0:512], src[:, k * 512:(k + 1) * 512], 'xin%d' % xi, writes=[('xin', xi)])
            b = bank()
            for c in range(4):
                transpose(ps[:, b, c * NSMP:(c + 1) * NSMP], xin[0:NSMP, xi, c * 128:(c + 1) * 128], [('xin', xi)],
                          [('ps', b)], np_in=NSMP)
            copy('dve', dst, ps[:, b, 0:4 * NSMP].rearrange("p (c t) -> p c t", t=NSMP), [('ps', b)], [key])

    def memory_kv():
        PHASE[0] = 'memkv'
        gi = VI["g_mem"]
        for j in range(2):
            if DEBUG.get('nomem1'):
                break
            dma('sp', xin[:, j, :], mem_d[j * 128:(j + 1) * 128, :], 'xin%d' % j, writes=[('xin', j)])
            ri = nxt('rs', NR)
            op('act', lambda e, j=j, ri=ri: e.activation(out=prod, in_=xin[:, j, :], func=AF.Square,
                                                        accum_out=rs[:, ri, 0:1]),
               reads=[('xin', j)], writes=PRODK + [('rs', ri)])
            ts('dve', rs[:, ri, 0:1], rs[:, ri, 0:1], 1.0 / D, EPS, ALU.mult, ALU.add, [('rs', ri)], [('rs', ri)])
            act(rs[:, ri, 0:1], rs[:, ri, 0:1], AF.Sqrt, [('rs', ri)], [('rs', ri)])
            op('dve', lambda e, ri=ri: e.reciprocal(out=rs[:, ri, 1:2], in_=rs[:, ri, 0:1]), reads=[('rs', ri)], writes=[('rs', ri)])
            ts('dve', xin[:, j, :], xin[:, j, :], rs[:, ri, 1:2], None, ALU.mult, None, [('xin', j), ('rs', ri)],
               [('xin', j)])
            for half in range(2):
                b = bank()
                for jj in range(4):
                    c = half * 4 + jj
                    transpose(ps[:, b, jj * 128:(jj + 1) * 128], xin[:, j, c * 128:(c + 1) * 128], [('xin', j)],
                              [('ps', b)])
                for jj in range(4):
                    c = half * 4 + jj
                    mt, mkey = memT(c)
                    ts('dve', mt[:, j * 128:(j + 1) * 128], ps[:, b, jj * 128:(jj + 1) * 128],
                       cst[:, c, gi:gi + 1], None, ALU.mult, None, [('ps', b), 'cst'], [mkey])
        if DEBUG.get('mkv') == 'A':
            slab_pos[0] = 16
            slab_issued[0] = 16
            return
        mnt = NT(0, 0, NM)

        def in_fn(kc, nt):
            return memT(kc)

        for (wn, out_d, isv) in (("xattn_wk", mk_d, False), ("xattn_wv", mv_d, True)):
            def ep(c, nt, banks, isv=isv):
                b = banks[0]
                k32, kkey = kt32(c)
                copy('act', k32, ps[:, b, :NM], [('ps', b)], [kkey])
                if not isv:
                    copy('dve', KT[:, c, :], ps[:, b, :NM], [('ps', b)], ['KT'])
            linear([[(wn, c)] for c in range(8)], 8, in_fn, [mnt], ep)
            if DEBUG.get('mkv') == 'B':
                continue
            for j in range(2):
                oi = nxt('xin', 2)
                for half in range(2):
                    b = bank()
                    for jj in range(4):
                        c = half * 4 + jj
                        k32, kkey = kt32(c)
                        transpose(ps[:, b, jj * 128:(jj + 1) * 128], k32[:, j * 128:(j + 1) * 128], [kkey], [('ps', b)])
                    copy('act', xin[:, oi, half * 512:(half + 1) * 512], ps[:, b, :], [('ps', b)], [('xin', oi)])
                    if isv:
                        copy('dve', Vn[:, j, half * 512:(half + 1) * 512], ps[:, b, :], [('ps', b)], ['Vn'])
                dma('sp', out_d[j * 128:(j + 1) * 128, :], xin[:, oi, :], 'xin%d' % oi, reads=[('xin', oi)])

    def memkv_slabs():
        return [("xattn_wk", c * 128, 8) for c in range(8)] + [("xattn_wv", c * 128, 8) for c in range(8)]

    def store_y(p, nt):
        PHASE[0] = 'p%s:store' % p
        for tcn in range(PT // 128):
            oi = nxt('xin', 2)
            col = tcn * 128
            for half in range(2):
                b = bank()
                for j in range(4):
                    c = half * 4 + j
                    transpose(ps[:, b, j * 128:(j + 1) * 128], xT[:, c, col:col + 128], [('xT', c, nt.i)], [('ps', b)])
                copy('act' if half == 0 else 'dve', xin[:, oi, half * 512:(half + 1) * 512], ps[:, b, :],
                     [('ps', b)], [('xin', oi)])
            t0 = p * PT + tcn * 128
            dma('sp', y_d[t0:t0 + 128, :], xin[:, oi, :], 'xin%d' % oi, reads=[('xin', oi)])

    def store_samples(nt):
        oi = nxt('xin', 2)
        for half in range(2):
            b = bank()
            for j in range(4):
                c = half * 4 + j
                transpose(ps[:NSMP, b, j * 128:(j + 1) * 128], xT[:, c, nt.cols], [('xT', c, nt.i)], [('ps', b)])
            copy('dve', xin[0:NSMP, oi, half * 512:(half + 1) * 512], ps[:NSMP, b, :], [('ps', b)], [('xin', oi)])
        dma('sp', ys_d[:, :], xin[0:NSMP, oi, :], 'xin%d' % oi, reads=[('xin', oi)])

    def store_sample_states():
        for base, dst in ((0, cas_d[:, 512:1024]), (4, cbs_d[:, 1024:1536]), (16, hs_d[:, :])):
            oi = nxt('xin', 2)
            b = bank()
            for c in range(4):
                transpose(ps[:NSMP, b, c * 128:(c + 1) * 128], Rs[:, base + c, :], [('Rs', base + c)], [('ps', b)])
            copy('dve', xin[0:NSMP, oi, 0:512], ps[:NSMP, b, :], [('ps', b)], [('xin', oi)])
            dma('sp', dst, xin[0:NSMP, oi, 0:512], 'xin%d' % oi, reads=[('xin', oi)])
        dma('sp', cas_d[:, 0:512], sca_d[:, 512:1024], 'ostore')
        dma('sp', cbs_d[:, 0:1024], scb_d[:, 512:1536], 'ostore')

    def store_prompt_states():
        for c in range(4):
            for (dst, src, key) in ((cap_d, carryA, 'carryA'), (cbp_d, carryB, 'carryB'), (hp_d, hcar, 'hcar')):
                op('sp', lambda e, c=c, dst=dst, src=src: e.dma_start(
                    out=dst[:, c * 128:(c + 1) * 128].rearrange("k p -> p k"), in_=src[:, c, :],
                    allow_slow_non_contiguous=True), reads=[(key, c)], dma_sem='ostore')
        dma_sems.add('ostore')

    stages = ['ffn1', 'mix', 'xattn', 'ffn2']
    nstage = len(stages) if stop is None else (stages.index(stop) + 1 if stop in stages else 0)
    per_pass = []
    if nstage >= 1:
        per_pass += ffn_slabs("ffn1_wg", "ffn1_wu", "ffn1_wd")
    if nstage >= 2:
        per_pass += mix_slabs()
    if nstage >= 3:
        per_pass += xattn_slabs()
    if nstage >= 4:
        per_pass += ffn_slabs("ffn2_wg", "ffn2_wu", "ffn2_wd")
    slabs.extend(memkv_slabs())
    for p in range(NPASS):
        slabs.extend(per_pass)

    if not DEBUG.get('nosetup'):
        setup()
    if stop != 'io0':
        memory_kv()
    else:
        slab_pos[0] = 16
        slab_issued[0] = 16
    for p in range(NPASS):
        if DEBUG.get('nopass'):
            break
        CURP[0] = p
        pnt = NT(0, 0, PT)
        nts = [pnt]
        last = (p == NPASS - 1)
        if last:
            snt = NT(1, PT, NSMP, sample=True)
            nts.append(snt)
        if p == 0:
            prefetch_x(0)
        load_x(p, pnt)
        if last:
            load_samples(snt)
        if nstage >= 1:
            ffn(nts, None, "ffn1_wg", "ffn1_wu", "ffn1_wd", "g_ffn1_pre", "g_ffn1_post")
        if nstage >= 2:
            mix(nts, p)
        if nstage >= 3:
            xattn(nts, p)
        if not last:
            prefetch_x(p + 1)
        if nstage >= 4:
            ffn(nts, None, "ffn2_wg", "ffn2_wu", "ffn2_wd", "g_ffn2_pre", "g_ffn2_post")
        store_y(p, pnt)
        if last:
            store_samples(snt)
            if nstage >= 2:
                store_sample_states()
    if nstage >= 2:
        store_prompt_states()
    assert slab_pos[0] == len(slabs), (slab_pos[0], len(slabs))

    semh = {}
    for name in list(S.ENG) + sorted(dma_sems):
        semh[name] = es.enter_context(nc.semaphore(name))
    hw = {'pe': 'tensor', 'act': 'scalar', 'dve': 'vector', 'pool': 'gpsimd', 'sp': 'sync'}

    def replay(name, e):
        for waits, fn, inc, ph in S.streams[name]:
            for s, v in waits:
                e.wait_ge(semh[s], v)
            ins = fn(e)
            if DEBUG.get('annot'):
                ins.annotate(ph)
            ins.then_inc(semh[inc[0]], inc[1])
        if name == 'sp':
            for s in sorted(dma_sems):
                if S.count.get(s, 0) > 0:
                    e.wait_ge(semh[s], S.count[s])
            for s in ('pe', 'act', 'dve', 'pool'):
                if S.count.get(s, 0) > 0:
                    e.wait_ge(semh[s], S.count[s])

    with nc.Block() as block:
        @block.tensor
        def _(e):
            replay('pe', e)

        @block.scalar
        def _(e):
            replay('act', e)

        @block.vector
        def _(e):
            replay('dve', e)

        @block.gpsimd
        def _(e):
            replay('pool', e)

        @block.sync
        def _(e):
            replay('sp', e)
    es.close()
    return nc


_W_NAMES = ["ffn1_wg", "ffn1_wu", "ffn1_wd", "w_in", "w_out", "xattn_wq", "xattn_wk", "xattn_wv", "xattn_wo",
            "ffn2_wg", "ffn2_wu", "ffn2_wd"]
_V_NAMES = ["g_ffn1_pre", "g_ffn1_post", "g_mix_pre", "g_mix_post", "g_xattn_pre", "g_xattn_post", "g_mem",
            "g_ffn2_pre", "g_ffn2_post", "conv_b_b", "lru_ba", "lru_bx", "lru_lam"]


def make_in_maps(inputs):
    f = lambda a: np.ascontiguousarray(np.asarray(a, dtype=np.float32))
    shared = {}
    for n in _W_NAMES:
        shared[n] = f(inputs[n][0])
    for n in _V_NAMES:
        shared[n] = f(inputs[n][0]).reshape(1, -1)
    shared["conv_a_w"] = f(inputs["conv_a_w"][0])
    shared["conv_b_w"] = f(inputs["conv_b_w"][0])
    shared["lru_wa"] = f(inputs["lru_wa"][0])
    shared["lru_wx"] = f(inputs["lru_wx"][0])
    maps = []
    for b in range(8):
        m = dict(shared)
        sl = slice(b * NSMP, (b + 1) * NSMP)
        m["x"] = f(inputs["x_prompt"][b])
        m["xs"] = f(inputs["x_sample"][sl, 0, :])
        m["mem"] = f(inputs["mem_prompt"][b])
        m["ck"] = f(inputs["cache_mem_k"][0, sl]).reshape(NSMP, NM, D)
        m["cv"] = f(inputs["cache_mem_v"][0, sl]).reshape(NSMP, NM, D)
        m["sca"] = f(inputs["state_conv_a"][0, sl]).reshape(NSMP, 1024)
        m["scb"] = f(inputs["state_conv_b"][0, sl]).reshape(NSMP, 1536)
        m["slru"] = f(inputs["state_lru"][0, sl]).reshape(NSMP, 512)
        maps.append(m)
    return maps


def assemble(results):
    cat = lambda k: np.stack([r[k] for r in results], axis=0)
    yp = cat("y")
    ys = np.concatenate([r["ys"] for r in results], axis=0).reshape(128, 1, D)
    mk = cat("mk").reshape(1, 8, NM, 4, 256)
    mv = cat("mv").reshape(1, 8, NM, 4, 256)
    cap = cat("cap").reshape(1, 8, 2, 512)
    cbp = cat("cbp").reshape(1, 8, 3, 512)
    hp = cat("hp").reshape(1, 8, 512)
    cas = np.concatenate([r["cas"] for r in results], axis=0).reshape(1, 128, 2, 512)
    cbs = np.concatenate([r["cbs"] for r in results], axis=0).reshape(1, 128, 3, 512)
    hs = np.concatenate([r["hs"] for r in results], axis=0).reshape(1, 128, 512)
    return tuple(np.ascontiguousarray(a, dtype=np.float32) for a in (yp, ys, mk, mv, cap, cbp, hp, cas, cbs, hs))


def kernel(**inputs):
    nc = build()
    maps = make_in_maps(inputs)
    res = run_bass_kernel_spmd(nc, maps, core_ids=list(range(8)))
    return assemble(res.results)
```

```python
import numpy as np
import concourse.bass as bass
import concourse.mybir as mybir
from concourse.bass_utils import run_bass_kernel_spmd
from contextlib import ExitStack

F32 = mybir.dt.float32
BF16 = mybir.dt.bfloat16
AF = mybir.ActivationFunctionType
ALU = mybir.AluOpType
AX = mybir.AxisListType

D = 1024
FF = 2816
T = 2048
PT = 512
NPASS = T // PT
NSMP = 16
NM = 256
W_ALL = PT + NSMP
NSLOT = 6
SLOT_ELEMS = 22 * 128
LOOKAHEAD = NSLOT - 1
EPS = 1e-6
NBANK = 5
K_PRE = 20
K_POST = 28
C0 = 0.7978845608028654
C1 = 0.044715
DEBUG = {}
PHASE = ['init']
CURP = ['-']
FLAT = True


class NT:
    def __init__(self, i, c0, w, sample=False):
        self.i = i
        self.c0 = c0
        self.w = w
        self.sample = sample
        self.cols = slice(c0, c0 + w)


class Sched:
    ENG = ('pe', 'act', 'dve', 'pool', 'sp')

    def __init__(self):
        self.streams = {e: [] for e in self.ENG}
        self.count = {}
        self.state = {}
        self.waited = {e: {} for e in self.ENG}

    def _deps(self, reads, writes):
        deps = {}

        def add(ev):
            if ev is None:
                return
            s, v = ev
            if deps.get(s, 0) < v:
                deps[s] = v
        for k in reads:
            st = self.state.get(k)
            if st:
                add(st[0])
        for k in writes:
            st = self.state.get(k)
            if st:
                add(st[0])
                for s, v in st[1].items():
                    add((s, v))
        return deps

    def op(self, eng, fn, reads=(), writes=(), dma_sem=None):
        psr = [k for k in reads if isinstance(k, tuple) and k[0] == 'ps']
        if psr:
            reads = [k for k in reads if not (isinstance(k, tuple) and k[0] == 'ps')]
            writes = list(writes) + psr
        deps = self._deps(reads, writes)
        waits = []
        for s, v in deps.items():
            if s == 'pe' and eng == 'pe':
                continue
            if self.waited[eng].get(s, 0) >= v:
                continue
            self.waited[eng][s] = v
            waits.append((s, v))
        if dma_sem is not None:
            self.count[dma_sem] = self.count.get(dma_sem, 0) + 16
            ev = (dma_sem, self.count[dma_sem])
            inc = (dma_sem, 16)
        else:
            self.count[eng] = self.count.get(eng, 0) + 1
            ev = (eng, self.count[eng])
            inc = (eng, 1)
        for k in reads:
            st = self.state.setdefault(k, [None, {}])
            if st[1].get(ev[0], 0) < ev[1]:
                st[1][ev[0]] = ev[1]
        for k in writes:
            self.state[k] = [ev, {}]
        self.streams[eng].append((waits, fn, inc, PHASE[0]))
        return ev


def build(stop=None):
    nc = bass.Bass("TRN2", target_bir_lowering=False)
    S = Sched()
    op = S.op

    def din(name, shape):
        return nc.dram_tensor(name, list(shape), F32, kind="ExternalInput").ap()

    def dout(name, shape):
        return nc.dram_tensor(name, list(shape), F32, kind="ExternalOutput").ap()

    x_d = din("x", [T, D])
    xs_d = din("xs", [NSMP, D])
    mem_d = din("mem", [NM, D])
    ck_d = din("ck", [NSMP, NM, D])
    cv_d = din("cv", [NSMP, NM, D])
    sca_d = din("sca", [NSMP, 2 * 512])
    scb_d = din("scb", [NSMP, 3 * 512])
    slru_d = din("slru", [NSMP, 512])
    vec1024 = ["g_ffn1_pre", "g_ffn1_post", "g_mix_pre", "g_mix_post", "g_xattn_pre", "g_xattn_post",
               "g_mem", "g_ffn2_pre", "g_ffn2_post"]
    vec_d = {n: din(n, [1, D]) for n in vec1024}
    caw_d = din("conv_a_w", [3, 512])
    cbw_d = din("conv_b_w", [4, 512])
    v512 = ["conv_b_b", "lru_ba", "lru_bx", "lru_lam"]
    v512_d = {n: din(n, [1, 512]) for n in v512}
    lwa_d = din("lru_wa", [8, 64, 64])
    lwx_d = din("lru_wx", [8, 64, 64])
    Wd = {}
    for n, shp in [("ffn1_wg", (D, FF)), ("ffn1_wu", (D, FF)), ("ffn1_wd", (FF, D)), ("w_in", (D, 2560)),
                   ("w_out", (D, D)), ("xattn_wq", (D, D)), ("xattn_wk", (D, D)), ("xattn_wv", (D, D)),
                   ("xattn_wo", (D, D)), ("ffn2_wg", (D, FF)), ("ffn2_wu", (D, FF)), ("ffn2_wd", (FF, D))]:
        Wd[n] = din(n, shp)

    y_d = dout("y", [T, D])
    ys_d = dout("ys", [NSMP, D])
    mk_d = dout("mk", [NM, D])
    mv_d = dout("mv", [NM, D])
    cap_d = dout("cap", [2, 512])
    cbp_d = dout("cbp", [3, 512])
    hp_d = dout("hp", [1, 512])
    cas_d = dout("cas", [NSMP, 2 * 512])
    cbs_d = dout("cbs", [NSMP, 3 * 512])
    hs_d = dout("hs", [NSMP, 512])

    es = ExitStack()

    def sb(name, shape, dt=F32):
        if not DEBUG.get('flat', FLAT):
            return es.enter_context(nc.sbuf_tensor(name, list(shape), dt))
        esz = 2 if dt == BF16 else 4
        n = 1
        for d_ in shape[1:]:
            n *= d_
        nbytes = n * esz
        assert nbytes % 4 == 0
        t = es.enter_context(nc.sbuf_tensor(name, [shape[0], nbytes // 4], F32))
        ap = t[:, :]
        if dt != F32:
            ap = ap.bitcast(dt)
        if len(shape) == 3:
            ap = ap.rearrange("p (a b) -> p a b", a=shape[1])
        elif len(shape) == 4:
            ap = ap.rearrange("p (a b c) -> p a b c", a=shape[1], b=shape[2])
        return ap

    ident = sb("ident", [128, 128])
    ones32 = sb("ones32", [128, 128])
    ones_bf = sb("ones_bf", [128, 128], BF16)
    epsb = sb("epsb", [128, 8])
    wrm = sb("wrm", [128, 512], BF16)
    xpre = sb("xpre", [128, 2, 1024])
    xT = sb("xT", [128, 8, W_ALL])
    hT = sb("hT", [128, 8, W_ALL], BF16)
    big = sb("big", [128, 22, W_ALL], BF16)
    ybuf = sb("ybuf", [128, 8, 1, 516])
    ybs = sb("ybs", [128, 8, NSMP])
    sq = sb("sq", [128, 4, 512], BF16)
    ring = sb("wslab", [128, NSLOT, SLOT_ELEMS], BF16)
    xin = sb("xin", [128, 2, 1024])
    NV = len(vec1024)
    cst = sb("cst", [128, 8, 32])
    cst5 = sb("cst5", [128, 4, 32])
    NR = 3
    rs = sb("rs", [128, NR, 512])
    NTMP = 8
    tmp = sb("tmp", [128, NTMP, 512])
    cb16 = sb("cb16", [128, 2, 512], BF16)
    bd = sb("bd", [128, 8, 128], BF16)
    carryA = sb("carryA", [128, 4, 2])
    carryB = sb("carryB", [128, 4, 3])
    hcar = sb("hcar", [128, 4, 1])
    KT = sb("KT", [128, 8, NM], BF16)
    Vn = sb("Vn", [128, 2, D], BF16)
    pT = sb("pT", [128, 2, 2, 512], BF16)
    rden = sb("rden", [128, 1, 512])
    Rs = sb("Rs", [128, 24, NSMP])
    SA = sb("SA", [128, 4, 2, NSMP])
    SB_ = sb("SB_", [128, 4, 3, NSMP])
    SH = sb("SH", [128, 4, NSMP])
    qs32 = sb("qs32", [128, 8, NSMP])
    qtok = sb("qtok", [NSMP, 1024], BF16)
    sel = sb("sel", [NSMP, NSMP, 128], BF16)
    Ks = sb("Ks", [128, 2, 2, D], BF16)
    Vs = sb("Vs", [128, 2, 2, D], BF16)
    sc = sb("sc", [128, 8])
    e16 = sb("e16", [128, 2, 8], BF16)
    rdens = sb("rdens", [128, NSMP, 4])
    c_lru = sb("c_lru", [128, 4, 4])
    ps = es.enter_context(nc.psum_tensor("ps", [128, 8, 512], F32))

    vstage = xin[0:32, 0, :]
    v5stage = xin[0:32, 1, 0:512]
    prod = tmp[:, 0:2, :].rearrange("p a b -> p (a b)")
    PRODK = [('tmp', 0), ('tmp', 1)]

    def kt32(c):
        return ybuf[:, c, 0, 0:NM], ('ybuf', c, 0)

    def memT(c):
        return hT[:, c, 0:NM], ('hT', c, 0)
    VI = {n: i for i, n in enumerate(vec1024)}
    GPOST = {"g_ffn1_post": 16, "g_ffn2_post": 17, "g_mix_post": VI["g_mix_post"], "g_xattn_post": VI["g_xattn_post"]}
    V5 = {"caw0": 0, "caw1": 1, "caw2": 2, "cbw0": 3, "cbw1": 4, "cbw2": 5, "cbw3": 6,
          "conv_b_b": 7, "lru_ba": 8, "lru_bx": 9, "lru_lam": 10}

    rot = {}

    def nxt(name, n):
        v = rot.get(name, 0)
        rot[name] = (v + 1) % n
        return v

    def bank():
        return nxt('bank', NBANK)

    dma_sems = set()

    def dma(q, out, in_, sem, reads=(), writes=()):
        dma_sems.add(sem)
        return op(q, lambda e, o=out, i=in_: e.dma_start(out=o, in_=i), reads=reads, writes=writes, dma_sem=sem)

    def act(out, in_, func, reads, writes, **kw):
        return op('act', lambda e: e.activation(out=out, in_=in_, func=func, **kw), reads=reads, writes=writes)

    def tt(eng, out, in0, in1, o, reads, writes):
        return op(eng, lambda e: e.tensor_tensor(out=out, in0=in0, in1=in1, op=o), reads=reads, writes=writes)

    def ts(eng, out, in0, s1, s2, o0, o1, reads, writes):
        if o1 is None:
            return op(eng, lambda e: e.tensor_scalar(out=out, in0=in0, scalar1=s1, scalar2=None, op0=o0),
                      reads=reads, writes=writes)
        return op(eng, lambda e: e.tensor_scalar(out=out, in0=in0, scalar1=s1, scalar2=s2, op0=o0, op1=o1),
                  reads=reads, writes=writes)

    def stt(out, in0, scalar, in1, o0, o1, reads, writes):
        return op('dve', lambda e: e.scalar_tensor_tensor(out=out, in0=in0, scalar=scalar, in1=in1, op0=o0, op1=o1),
                  reads=reads, writes=writes)

    def copy(eng, out, in_, reads, writes):
        if eng == 'act':
            return act(out, in_, AF.Copy, reads, writes)
        return op(eng, lambda e: e.tensor_copy(out=out, in_=in_), reads=reads, writes=writes)

    def memset(eng, ap, val, writes):
        return op(eng, lambda e: e.memset(ap, val), writes=writes)

    def transpose(out, in_, reads, writes, np_in=128):
        return op('pe', lambda e: e.transpose(out, in_, ident[:np_in, :np_in]), reads=list(reads) + ['ident'],
                  writes=writes)

    slabs = []
    slab_pos = [0]
    slab_issued = [0]

    def issue_upto(j):
        while slab_issued[0] <= j and slab_issued[0] < len(slabs):
            i = slab_issued[0]
            wname, col0, nk = slabs[i]
            slot = i % NSLOT
            src = Wd[wname].rearrange("(kc p) n -> p kc n", p=128)[:, :, col0:col0 + 128]
            if DEBUG.get('srcx'):
                src = x_d[0:1024, :].rearrange("(kc p) n -> p kc n", p=128)[:, :, col0:col0 + 128]
            dst = ring[:, slot, 0:nk * 128].rearrange("p (kc n) -> p kc n", n=128)
            dma('pool', dst, src, ('ring' if DEBUG.get('onesem') else 'ring%d' % slot), writes=[('ring', slot)])
            if DEBUG.get('serial'):
                op('pool', lambda e: e.memset(sc[:, 0:1], 0.0), reads=[('ring', slot)], writes=['scdummy'])
            slab_issued[0] += 1

    def next_slab(wname, col0, nk, ahead=LOOKAHEAD):
        i = slab_pos[0]
        assert slabs[i] == (wname, col0, nk), (i, slabs[i], wname, col0, nk)
        issue_upto(i + ahead)
        slab_pos[0] += 1
        return i % NSLOT

    def linear(jobs, nk, in_fn, nts, epilogue):
        PHASE[0] = "p%s:lin:%s:%d" % (CURP[0], jobs[0][0][0], jobs[0][0][1])
        for ji, job in enumerate(jobs):
            nj = len(job)
            slots = [next_slab(wn, cc * 128, nk, LOOKAHEAD - (nj - 1) - j) for j, (wn, cc) in enumerate(job)]
            if DEBUG.get('lin') == 'dma':
                continue
            for nt in nts:
                banks = []
                for slot in slots:
                    b = bank()
                    banks.append(b)
                    ins = [in_fn(kc, nt) for kc in range(nk)]

                    def fn(e, slot=slot, b=b, ins=ins, w=nt.w):
                        last = None
                        for kc in range(nk):
                            last = e.matmul(ps[:, b, :w], lhsT=ring[:, slot, kc * 128:(kc + 1) * 128],
                                            rhs=ins[kc][0], start=(kc == 0), stop=(kc == nk - 1))
                        return last
                    op('pe', fn, reads=[('ring', slot)] + [k for _, k in ins], writes=[('ps', b)])
                if DEBUG.get('lin') != 'mm':
                    epilogue(ji, nt, banks)

    def rstd_from_psum(b, w):
        ri = nxt('rs', NR)
        act(rs[:, ri, :w], ps[:, b, :w], AF.Ln, [('ps', b), 'consts'], [('rs', ri)], scale=1.0 / D, bias=epsb[:, 0:1])
        act(rs[:, ri, :w], rs[:, ri, :w], AF.Exp, [('rs', ri)], [('rs', ri)], scale=-0.5)
        return ri

    def warm(k):
        if k <= 0 or DEBUG.get('nowarm'):
            return

        def fn(e):
            last = None
            for _ in range(k):
                last = e.matmul(ps[:, 5, :], lhsT=ones_bf[:], rhs=wrm[:, :], start=True, stop=True)
            return last
        op('pe', fn, reads=['consts'], writes=[('ps', 5)])

    def norm_stats(src_fn, w):
        b = bank()
        for c in range(8):
            ap, key = src_fn(c)
            si = nxt('sq', 4)
            act(sq[:, si, :w], ap, AF.Square, [key], [('sq', si)])
            op('pe', lambda e, si=si, c=c: e.matmul(ps[:, b, :w], lhsT=ones_bf[:], rhs=sq[:, si, :w],
                                                   start=(c == 0), stop=(c == 7)),
               reads=[('sq', si), 'consts'], writes=[('ps', b)])
        return rstd_from_psum(b, w)

    def xkey(c, nt):
        return ('xT', c, nt.i)

    def prenorm(nt, gname):
        PHASE[0] = 'p%s:pre:%s' % (CURP[0], gname)
        gi = VI[gname]
        ri = norm_stats(lambda c: (xT[:, c, nt.cols], xkey(c, nt)), nt.w)
        if not nt.sample:
            warm(K_PRE)
        for c in range(8):
            stt(hT[:, c, nt.cols], xT[:, c, nt.cols], cst[:, c, gi:gi + 1], rs[:, ri, :nt.w], ALU.mult, ALU.mult,
                [xkey(c, nt), ('rs', ri), 'cst'], [('hT', c, nt.i)])

    def yb_ap(c, nt):
        if nt.sample:
            return ybs[:, c, :], ('ybs', c)
        return ybuf[:, c, nt.i, 4:4 + nt.w], ('ybuf', c, nt.i)

    def stat_bank(nt):
        return 6 if nt.sample else 7

    def y_epilogue(c, nt, b, gname):
        gi = GPOST[gname]
        ap, key = yb_ap(c, nt)
        w = nt.w
        sbk = stat_bank(nt)
        si = nxt('sq', 4)
        act(sq[:, si, :w], ps[:, b, :w], AF.Square, [('ps', b)], [('sq', si)])
        act(ap, ps[:, b, :w], AF.Copy, [('ps', b), 'cst'], [key], scale=cst[:, c, gi:gi + 1])
        op('pe', lambda e: e.matmul(ps[:, sbk, :w], lhsT=ones_bf[:], rhs=sq[:, si, :w], start=(c == 0), stop=(c == 7)),
           reads=[('sq', si), 'consts'], writes=[('ps', sbk)])

    def postnorm(nt, gname, coef):
        PHASE[0] = 'p%s:post:%s' % (CURP[0], gname)
        w = nt.w
        if not nt.sample:
            warm(K_POST)
        ri = rstd_from_psum(stat_bank(nt), w)
        for c in range(8):
            ap, key = yb_ap(c, nt)
            ti = nxt('tmp', NTMP)
            tt('dve', tmp[:, ti, :w], ap, rs[:, ri, :w], ALU.mult, [key, ('rs', ri)], [('tmp', ti)])
            tt('dve', xT[:, c, nt.cols], xT[:, c, nt.cols], tmp[:, ti, :w],
               ALU.add, [('tmp', ti), xkey(c, nt)], [xkey(c, nt)])

    def hT_in(kc, nt):
        return hT[:, kc, nt.cols], ('hT', kc, nt.i)

    def ffn(nts, pre, wg, wu, wd, gpre, gpost):
        for nt in nts:
            prenorm(nt, gpre)

        def ep1(m, nt, banks):
            bg, bu = banks
            ti = nxt('tmp', NTMP)
            act(tmp[:, ti, :nt.w], ps[:, bg, :nt.w], AF.Silu, [('ps', bg)], [('tmp', ti)])
            tt('dve', big[:, m, nt.cols], tmp[:, ti, :nt.w], ps[:, bu, :nt.w], ALU.mult,
               [('tmp', ti), ('ps', bu)], [('big', m, nt.i)])
        linear([[(wg, m), (wu, m)] for m in range(22)], 8, hT_in, nts, ep1)

        def ep2(c, nt, banks):
            y_epilogue(c, nt, banks[0], gpost)
        linear([[(wd, c)] for c in range(8)], 22, lambda kc, nt: (big[:, kc, nt.cols], ('big', kc, nt.i)), nts, ep2)
        for nt in nts:
            postnorm(nt, gpost, 0.5)

    def ffn_slabs(wg, wu, wd):
        out = []
        for m in range(22):
            out += [(wg, m * 128, 8), (wu, m * 128, 8)]
        out += [(wd, c * 128, 22) for c in range(8)]
        return out

    def R(i, nt):
        if nt.sample:
            return None
        if i < 8:
            return ybuf[:, i, nt.i, :], [('ybuf', i, nt.i)]
        ch = 8 + 2 * (i - 8)
        return (big[:, ch:ch + 2, :].rearrange("p a b -> p (a b)").bitcast(F32)[:, 0:516],
                [('big', ch, 0), ('big', ch + 1, 0)])

    def mix(nts, p):
        for nt in nts:
            prenorm(nt, "g_mix_pre")
        ptiles = [nt for nt in nts if not nt.sample]
        stiles = [nt for nt in nts if nt.sample]

        def c5(c, name):
            i = V5[name]
            return cst5[:, c, i:i + 1]

        def ep_xb(c, nt, banks):
            b = banks[0]
            if nt.sample:
                copy('act', Rs[:, 4 + c, :], ps[:, b, :NSMP], [('ps', b)], [('Rs', 4 + c)])
                return
            r, keys = R(4 + c, nt)
            copy('dve', r[:, 1:4], carryB[:, c, :], [('carryB', c)], keys)
            copy('act', r[:, 4:516], ps[:, b, :512], [('ps', b)], keys)
            copy('dve', carryB[:, c, :], r[:, 513:516], keys, [('carryB', c)])
        linear([[("w_in", 12 + c)] for c in range(4)], 8, hT_in, nts, ep_xb)

        def conv_b(c, nt):
            if nt.sample:
                o = Rs[:, 8 + c, :]
                ts('dve', o, SB_[:, c, 0, :], c5(c, "cbw0"), c5(c, "conv_b_b"), ALU.mult, ALU.add,
                   ['SB_', 'cst5'], [('Rs', 8 + c)])
                for k in (1, 2):
                    stt(o, SB_[:, c, k, :], c5(c, "cbw%d" % k), o, ALU.mult, ALU.add, ['SB_', 'cst5', ('Rs', 8 + c)],
                        [('Rs', 8 + c)])
                stt(o, Rs[:, 4 + c, :], c5(c, "cbw3"), o, ALU.mult, ALU.add, [('Rs', 4 + c), 'cst5', ('Rs', 8 + c)],
                    [('Rs', 8 + c)])
                return
            x_, xk = R(4 + c, nt)
            cbr, ck = R(10 + c, nt)
            o = cbr[:, 4:516]
            ts('dve', o, x_[:, 1:513], c5(c, "cbw0"), c5(c, "conv_b_b"), ALU.mult, ALU.add, xk + ['cst5'], ck)
            for k in (1, 2, 3):
                stt(o, x_[:, 1 + k:513 + k], c5(c, "cbw%d" % k), o, ALU.mult, ALU.add, xk + ck + ['cst5'], ck)

        for c in range(4):
            for nt in nts:
                conv_b(c, nt)

        def ep_A(which):
            def ep(c_rel, nt, banks):
                c = ep.c0 + c_rel // 3
                role = c_rel % 3
                b = banks[0]
                w = nt.w
                if nt.sample:
                    if role == 0:
                        copy('act', Rs[:, c, :], ps[:, b, :w], [('ps', b)], [('Rs', c)])
                    elif role == 1:
                        tt('dve', Rs[:, c, :], Rs[:, c, :], ps[:, b, :w], ALU.mult, [('Rs', c), ('ps', b)], [('Rs', c)])
                        o = Rs[:, 12 + c, :]
                        ts('dve', o, SA[:, c, 0, :], c5(c, "caw0"), None, ALU.mult, None, ['SA', 'cst5'],
                           [('Rs', 12 + c)])
                        stt(o, SA[:, c, 1, :], c5(c, "caw1"), o, ALU.mult, ALU.add, ['SA', 'cst5', ('Rs', 12 + c)],
                            [('Rs', 12 + c)])
                        stt(o, Rs[:, c, :], c5(c, "caw2"), o, ALU.mult, ALU.add, [('Rs', c), 'cst5', ('Rs', 12 + c)],
                            [('Rs', 12 + c)])
                    else:
                        tt('dve', big[:, c, nt.cols], Rs[:, 12 + c, :], ps[:, b, :w], ALU.mult,
                           [('Rs', 12 + c), ('ps', b)], [('big', c, nt.i)])
                    return
                v_, vk = R(c, nt)
                ca_, cak = R(8 + (c % 2), nt)
                if role == 0:
                    copy('dve', v_[:, 2:4], carryA[:, c, :], [('carryA', c)], vk)
                    copy('act', v_[:, 4:516], ps[:, b, :512], [('ps', b)], vk)
                elif role == 1:
                    tt('dve', v_[:, 4:516], v_[:, 4:516], ps[:, b, :512], ALU.mult, vk + [('ps', b)], vk)
                    copy('dve', carryA[:, c, :], v_[:, 514:516], vk, [('carryA', c)])
                    o = ca_[:, 4:516]
                    ts('dve', o, v_[:, 2:514], c5(c, "caw0"), None, ALU.mult, None, vk + ['cst5'], cak)
                    stt(o, v_[:, 3:515], c5(c, "caw1"), o, ALU.mult, ALU.add, vk + cak + ['cst5'], cak)
                    stt(o, v_[:, 4:516], c5(c, "caw2"), o, ALU.mult, ALU.add, vk + cak + ['cst5'], cak)
                else:
                    tt('dve', big[:, c, nt.cols], ca_[:, 4:516], ps[:, b, :512], ALU.mult, cak + [('ps', b)],
                       [('big', c, nt.i)])
            ep.c0 = which
            return ep

        def A_jobs(c0, n):
            jobs = []
            for c in range(c0, c0 + n):
                jobs += [[("w_in", 8 + c)], [("w_in", 4 + c)], [("w_in", c)]]
            return jobs
        linear(A_jobs(0, 2), 8, hT_in, nts, ep_A(0))

        def lru(c, nt):
            PHASE[0] = 'p%s:lru' % CURP[0]
            w = nt.w
            if nt.sample:
                cbv = Rs[:, 8 + c, :]
                cbk = [('Rs', 8 + c)]
            else:
                cbr, cbk = R(10 + c, nt)
                cbv = cbr[:, 4:516]
            ci = nxt('cb16', 2)
            copy('act', cb16[:, ci, :w], cbv, cbk, [('cb16', ci)])
            ba, bx = bank(), bank()
            op('pe', lambda e: e.matmul(ps[:, ba, :w], lhsT=bd[:, c, :], rhs=cb16[:, ci, :w], start=True, stop=True),
               reads=[('cb16', ci), 'bd'], writes=[('ps', ba)])
            op('pe', lambda e: e.matmul(ps[:, bx, :w], lhsT=bd[:, 4 + c, :], rhs=cb16[:, ci, :w], start=True, stop=True),
               reads=[('cb16', ci), 'bd'], writes=[('ps', bx)])
            t_a, t_x, t_aa, t_m = [nxt('tmp', NTMP) for _ in range(4)]
            act(tmp[:, t_a, :w], ps[:, ba, :w], AF.Tanh, [('ps', ba), 'c_lru'], [('tmp', t_a)],
                scale=0.5, bias=c_lru[:, c, 2:3])
            act(tmp[:, t_x, :w], ps[:, bx, :w], AF.Tanh, [('ps', bx), 'c_lru'], [('tmp', t_x)],
                scale=0.5, bias=c_lru[:, c, 3:4])
            act(tmp[:, t_aa, :w], tmp[:, t_a, :w], AF.Exp, [('tmp', t_a), 'c_lru'], [('tmp', t_aa)],
                scale=c_lru[:, c, 0:1], bias=c_lru[:, c, 0:1])
            act(tmp[:, t_m, :w], tmp[:, t_a, :w], AF.Exp, [('tmp', t_a), 'c_lru'], [('tmp', t_m)],
                scale=c_lru[:, c, 1:2], bias=c_lru[:, c, 1:2])
            ts('dve', tmp[:, t_m, :w], tmp[:, t_m, :w], 0.9999999, -1.0, ALU.min, ALU.mult, [('tmp', t_m)], [('tmp', t_m)])
            act(tmp[:, t_m, :w], tmp[:, t_m, :w], AF.Sqrt, [('tmp', t_m), 'consts'], [('tmp', t_m)], bias=epsb[:, 1:2])
            if (not nt.sample) and p == 0 and nt.i == 0:
                memset('dve', tmp[:, t_m, 0:1], 1.0, [('tmp', t_m)])
            stt(tmp[:, t_x, :w], tmp[:, t_x, :w], 1.0, cbv, ALU.add, ALU.mult, [('tmp', t_x)] + cbk, [('tmp', t_x)])
            stt(tmp[:, t_x, :w], tmp[:, t_x, :w], 0.5, tmp[:, t_m, :w], ALU.mult, ALU.mult,
                [('tmp', t_x), ('tmp', t_m)], [('tmp', t_x)])
            if nt.sample:
                hb = Rs[:, 16 + c, :]
                tt('dve', hb, tmp[:, t_aa, :w], SH[:, c, :], ALU.mult, [('tmp', t_aa), 'SH'], [('Rs', 16 + c)])
                tt('dve', hb, hb, tmp[:, t_x, :w], ALU.add, [('Rs', 16 + c), ('tmp', t_x)], [('Rs', 16 + c)])
            else:
                hbr, hk = R(10 + c, nt)
                op('dve', lambda e: e.tensor_tensor_scan(out=hbr[:, 4:516], data0=tmp[:, t_aa, :w],
                                                         data1=tmp[:, t_x, :w], initial=hcar[:, c, :],
                                                         op0=ALU.mult, op1=ALU.add),
                   reads=[('tmp', t_aa), ('tmp', t_x), ('hcar', c)], writes=hk)
                copy('dve', hcar[:, c, :], hbr[:, 515:516], hk, [('hcar', c)])

        for c in range(4):
            for nt in nts:
                lru(c, nt)

        linear(A_jobs(2, 2), 8, hT_in, nts, ep_A(2))

        def ep_gg(c, nt, banks):
            b = banks[0]
            w = nt.w
            g = ps[:, b, :w]
            if nt.sample:
                hv, hk = Rs[:, 16 + c, :], [('Rs', 16 + c)]
            else:
                hbr, hk = R(10 + c, nt)
                hv = hbr[:, 4:516]
            t1, t2 = nxt('tmp', NTMP), nxt('tmp', NTMP)
            act(tmp[:, t1, :w], g, AF.Square, [('ps', b)], [('tmp', t1)], scale=float(np.sqrt(C1)))
            stt(tmp[:, t1, :w], tmp[:, t1, :w], 1.0, g, ALU.add, ALU.mult, [('tmp', t1), ('ps', b)], [('tmp', t1)])
            act(tmp[:, t2, :w], tmp[:, t1, :w], AF.Tanh, [('tmp', t1)], [('tmp', t2)], scale=C0)
            stt(tmp[:, t2, :w], tmp[:, t2, :w], 1.0, g, ALU.add, ALU.mult, [('tmp', t2), ('ps', b)], [('tmp', t2)])
            stt(big[:, 4 + c, nt.cols], tmp[:, t2, :w], 0.5, hv, ALU.mult, ALU.mult, [('tmp', t2)] + hk,
                [('big', 4 + c, nt.i)])
        linear([[("w_in", 16 + c)] for c in range(4)], 8, hT_in, nts, ep_gg)

        def ep2(c, nt, banks):
            y_epilogue(c, nt, banks[0], "g_mix_post")
        linear([[("w_out", c)] for c in range(8)], 8, lambda kc, nt: (big[:, kc, nt.cols], ('big', kc, nt.i)), nts, ep2)
        for nt in nts:
            postnorm(nt, "g_mix_post", 1.0)

    def mix_slabs():
        out = [("w_in", (12 + c) * 128, 8) for c in range(4)]
        for c in range(4):
            out += [("w_in", (8 + c) * 128, 8), ("w_in", (4 + c) * 128, 8), ("w_in", c * 128, 8)]
        out += [("w_in", (16 + c) * 128, 8) for c in range(4)]
        out += [("w_out", c * 128, 8) for c in range(8)]
        return out

    def prompt_attn(nt):
        PHASE[0] = 'p%s:pattn' % CURP[0]
        w = nt.w
        for h in range(4):
            pi = nxt('pT', 2)
            for mc in range(2):
                b = bank()

                def fn(e, b=b, mc=mc, h=h):
                    e.matmul(ps[:, b, :w], lhsT=KT[:, 2 * h, mc * 128:(mc + 1) * 128], rhs=big[:, 2 * h, nt.cols],
                             start=True, stop=False)
                    return e.matmul(ps[:, b, :w], lhsT=KT[:, 2 * h + 1, mc * 128:(mc + 1) * 128],
                                    rhs=big[:, 2 * h + 1, nt.cols], start=False, stop=True)
                op('pe', fn, reads=['KT', ('big', 2 * h, nt.i), ('big', 2 * h + 1, nt.i)], writes=[('ps', b)])
                act(pT[:, pi, mc, :w], ps[:, b, :w], AF.Exp, [('ps', b)], [('pT', pi, mc)], scale=1.0 / 16.0)
            bden = bank()

            def fnd(e, bden=bden, pi=pi):
                e.matmul(ps[:, bden, :w], lhsT=ones_bf[:], rhs=pT[:, pi, 0, :w], start=True, stop=False)
                return e.matmul(ps[:, bden, :w], lhsT=ones_bf[:], rhs=pT[:, pi, 1, :w], start=False, stop=True)
            op('pe', fnd, reads=[('pT', pi, 0), ('pT', pi, 1), 'consts'], writes=[('ps', bden)])
            act(rden[:, 0, :w], ps[:, bden, :w], AF.Ln, [('ps', bden)], ['rden'])
            act(rden[:, 0, :w], rden[:, 0, :w], AF.Exp, ['rden'], ['rden'], scale=-1.0)
            for ee in range(2):
                bo = bank()
                cc = 2 * h + ee

                def fno(e, bo=bo, cc=cc, pi=pi):
                    e.matmul(ps[:, bo, :w], lhsT=Vn[:, 0, cc * 128:(cc + 1) * 128], rhs=pT[:, pi, 0, :w],
                             start=True, stop=False)
                    return e.matmul(ps[:, bo, :w], lhsT=Vn[:, 1, cc * 128:(cc + 1) * 128], rhs=pT[:, pi, 1, :w],
                                    start=False, stop=True)
                op('pe', fno, reads=['Vn', ('pT', pi, 0), ('pT', pi, 1)], writes=[('ps', bo)])
                tt('dve', big[:, 8 + cc, nt.cols], ps[:, bo, :w], rden[:, 0, :w], ALU.mult,
                   [('ps', bo), 'rden'], [('big', 8 + cc, nt.i)])

    def xattn(nts, p):
        for nt in nts:
            prenorm(nt, "g_xattn_pre")

        def ep_q(c, nt, banks):
            b = banks[0]
            copy('act', big[:, c, nt.cols], ps[:, b, :nt.w], [('ps', b)], [('big', c, nt.i)])
            if nt.sample:
                copy('dve', qs32[:, c, :], ps[:, b, :nt.w], [('ps', b)], [('qs32', c)])
        linear([[("xattn_wq", c)] for c in range(8)], 8, hT_in, nts, ep_q)

        for nt in nts:
            if nt.sample:
                sample_attn(nt)
            else:
                prompt_attn(nt)

        def ep2(c, nt, banks):
            y_epilogue(c, nt, banks[0], "g_xattn_post")
        linear([[("xattn_wo", c)] for c in range(8)], 8,
               lambda kc, nt: (big[:, 8 + kc, nt.cols], ('big', 8 + kc, nt.i)), nts, ep2)
        for nt in nts:
            postnorm(nt, "g_xattn_post", 1.0)

    def xattn_slabs():
        return [("xattn_wq", c * 128, 8) for c in range(8)] + [("xattn_wo", c * 128, 8) for c in range(8)]

    def sample_attn(nt):
        PHASE[0] = 'p%s:sattn' % CURP[0]
        for half in range(2):
            b = bank()
            for j in range(4):
                c = half * 4 + j
                transpose(ps[:NSMP, b, j * 128:(j + 1) * 128], qs32[:, c, :], [('qs32', c)], [('ps', b)])
            copy('act', qtok[:, half * 512:(half + 1) * 512], ps[:NSMP, b, :], [('ps', b)], [('qtok', half)])
        BO = 7
        for s in range(NSMP):
            ri = s % 2
            src_k = ck_d[s].rearrange("(mc p) d -> p mc d", p=128)
            src_v = cv_d[s].rearrange("(mc p) d -> p mc d", p=128)
            dma('pool', Ks[:, ri, :, :], src_k, 'ks%d' % ri, writes=[('Ks', ri)])
            dma('pool', Vs[:, ri, :, :], src_v, 'vs%d' % ri, writes=[('Vs', ri)])
            b0 = nxt('bank', NBANK)
            while b0 == NBANK - 1:
                b0 = nxt('bank', NBANK)
            b1 = nxt('bank', NBANK)
            assert b1 == b0 + 1
            for hh, bb in ((0, b0), (1, b1)):
                op('pe', lambda e, hh=hh, bb=bb, s=s: e.matmul(ps[:, bb, :], lhsT=sel[:, s, :],
                                                              rhs=qtok[:, hh * 512:(hh + 1) * 512],
                                                              start=True, stop=True),
                   reads=[('qtok', hh), 'sel'], writes=[('ps', bb)])
            qbc = ps[:, b0:b0 + 2, :].rearrange("p a b -> p (a b)")
            for mc in range(2):
                tt('dve', prod, Ks[:, ri, mc, :], qbc, ALU.mult, [('Ks', ri), ('ps', b0), ('ps', b1)], PRODK)
                op('dve', lambda e, mc=mc: e.tensor_reduce(out=sc[:, mc * 4:(mc + 1) * 4],
                                                          in_=prod.rearrange("p (h d) -> p h d", d=256),
                                                          axis=AX.X, op=ALU.add),
                   reads=PRODK, writes=[('sc', mc)])
            ei = nxt('e16', 2)
            act(e16[:, ei, :], sc[:, :], AF.Exp, [('sc', 0), ('sc', 1)], [('e16', ei)], scale=1.0 / 16.0)
            bden = bank()
            op('pe', lambda e, bden=bden, ei=ei: e.matmul(ps[:, bden, 0:8], lhsT=ones_bf[:], rhs=e16[:, ei, :],
                                                         start=True, stop=True),
               reads=[('e16', ei), 'consts'], writes=[('ps', bden)])
            op('dve', lambda e, bden=bden, s=s: e.tensor_reduce(
                out=rdens[:, s, :], in_=ps[:, bden, 0:8].rearrange("p (mc h) -> p h mc", mc=2),
                axis=AX.X, op=ALU.add), reads=[('ps', bden)], writes=[('rdens', s)])
            op('dve', lambda e, s=s: e.reciprocal(out=rdens[:, s, :], in_=rdens[:, s, :]),
               reads=[('rdens', s)], writes=[('rdens', s)])

            def fpv(e, s=s, ri=ri, ei=ei):
                last = None
                for c in range(8):
                    h = c // 2
                    for mc in range(2):
                        last = e.matmul(ps[:, BO, c * NSMP + s:c * NSMP + s + 1],
                                        lhsT=Vs[:, ri, mc, c * 128:(c + 1) * 128],
                                        rhs=e16[:, ei, mc * 4 + h:mc * 4 + h + 1], start=(mc == 0), stop=(mc == 1))
                return last
            op('pe', fpv, reads=[('Vs', ri), ('e16', ei)], writes=[('ps', BO)])
        for c in range(8):
            h = c // 2
            tt('dve', big[:, 8 + c, nt.cols], ps[:, BO, c * NSMP:(c + 1) * NSMP], rdens[:, :, h], ALU.mult,
               [('ps', BO)] + [('rdens', s) for s in range(NSMP)], [('big', 8 + c, nt.i)])

    XK0, XK1 = ('xin', 0), ('xin', 1)

    def setup():
        memset('dve', ones32[:], 1.0, ['ones32'])
        op('pool', lambda e: e.affine_select(out=ident[:], in_=ones32[:], pattern=[[1, 128]],
                                             compare_op=ALU.is_equal, fill=0.0, base=0, channel_multiplier=-1),
           reads=['ones32'], writes=['ident'])
        memset('dve', ones_bf[:], 1.0, ['consts'])
        memset('dve', epsb[:], EPS, ['consts'])
        memset('dve', epsb[:, 1:2], 1.0, ['consts'])
        memset('dve', wrm[:], 0.37, ['consts'])
        memset('dve', carryA[:], 0.0, [('carryA', c) for c in range(4)])
        memset('dve', carryB[:], 0.0, [('carryB', c) for c in range(4)])
        memset('dve', hcar[:], 0.0, [('hcar', c) for c in range(4)])
        memset('dve', bd[:], 0.0, ['bd'])
        memset('dve', xin[0:32, 0, :], 0.0, [XK0])
        memset('dve', xin[0:32, 1, :], 0.0, [XK1])
        for n in vec1024:
            dma('sp', vstage[VI[n]:VI[n] + 1, :], vec_d[n][0:1, :], 'cload', writes=[XK0])
        dma('sp', v5stage[0:3, :], caw_d[:, :], 'cload5', writes=[XK1])
        dma('sp', v5stage[3:7, :], cbw_d[:, :], 'cload5', writes=[XK1])
        for n in v512:
            dma('sp', v5stage[V5[n]:V5[n] + 1, :], v512_d[n][0:1, :], 'cload5', writes=[XK1])
        for g, wd_ in ((0, lwa_d), (1, lwx_d)):
            if DEBUG.get('nobd'):
                break
            for h in range(8):
                c, r = h // 2, h % 2
                dma('pool', bd[r * 64:(r + 1) * 64, g * 4 + c, r * 64:(r + 1) * 64], wd_[h], 'bdload',
                    writes=['bd'])
        for half in range(2):
            b = bank()
            for j in range(4):
                c = half * 4 + j
                transpose(ps[:, b, j * 32:(j + 1) * 32], vstage[:, c * 128:(c + 1) * 128], [XK0], [('ps', b)],
                          np_in=32)
            copy('dve', cst[:, half * 4:(half + 1) * 4, :], ps[:, b, 0:128].rearrange("p (j v) -> p j v", v=32),
                 [('ps', b)], ['cst'])
        b = bank()
        for c in range(4):
            transpose(ps[:, b, c * 32:(c + 1) * 32], v5stage[:, c * 128:(c + 1) * 128], [XK1], [('ps', b)],
                      np_in=32)
        copy('dve', cst5[:, :, :], ps[:, b, 0:128].rearrange("p (j v) -> p j v", v=32), [('ps', b)], ['cst5'])
        ts('dve', cst[:, :, 16], cst[:, :, VI["g_ffn1_post"]], 0.5, None, ALU.mult, None, ['cst'], ['cst'])
        ts('dve', cst[:, :, 17], cst[:, :, VI["g_ffn2_post"]], 0.5, None, ALU.mult, None, ['cst'], ['cst'])
        il, iba, ibx = V5["lru_lam"], V5["lru_ba"], V5["lru_bx"]
        K = ['c_lru']
        act(c_lru[:, :, 0], cst5[:, :, il], AF.Exp, ['cst5'], K, scale=-1.0)
        ts('dve', c_lru[:, :, 1], c_lru[:, :, 0], 1.0 / 3.0, -0.5, ALU.mult, ALU.add, K, K)
        tt('dve', c_lru[:, :, 1], c_lru[:, :, 1], c_lru[:, :, 0], ALU.mult, K, K)
        ts('dve', c_lru[:, :, 1], c_lru[:, :, 1], 1.0, None, ALU.add, None, K, K)
        tt('dve', c_lru[:, :, 1], c_lru[:, :, 1], c_lru[:, :, 0], ALU.mult, K, K)
        ts('dve', c_lru[:, :, 0], c_lru[:, :, 1], -4.0, None, ALU.mult, None, K, K)
        ts('dve', c_lru[:, :, 1], c_lru[:, :, 1], -8.0, None, ALU.mult, None, K, K)
        ts('dve', c_lru[:, :, 2], cst5[:, :, iba], 0.5, None, ALU.mult, None, ['cst5'] + K, K)
        ts('dve', c_lru[:, :, 3], cst5[:, :, ibx], 0.5, None, ALU.mult, None, ['cst5'] + K, K)
        copy('dve', sel[:, :, :], ident[:NSMP, :NSMP].unsqueeze(2).to_broadcast([NSMP, NSMP, 128]), ['ident'], ['sel'])

    def prefetch_x(p):
        for tcn in range(2):
            t0 = p * PT + tcn * 128
            dma('sp', xpre[:, tcn, :], x_d[t0:t0 + 128, :], 'xpre%d' % tcn, writes=[('xpre', tcn)])

    def load_x(p, nt):
        PHASE[0] = 'p%s:load' % p
        for tcn in range(PT // 128):
            t0 = p * PT + tcn * 128
            if tcn < 2:
                src, skey = xpre[:, tcn, :], ('xpre', tcn)
            else:
                xi = nxt('xin', 2)
                dma('sp', xin[:, xi, :], x_d[t0:t0 + 128, :], 'xin%d' % xi, writes=[('xin', xi)])
                src, skey = xin[:, xi, :], ('xin', xi)
            col = tcn * 128
            for half in range(2):
                b = bank()
                for j in range(4):
                    c = half * 4 + j
                    transpose(ps[:, b, j * 128:(j + 1) * 128], src[:, c * 128:(c + 1) * 128], [skey],
                              [('ps', b)])
                copy('act' if half == 0 else 'dve', xT[:, half * 4:(half + 1) * 4, col:col + 128],
                     ps[:, b, :].rearrange("p (j t) -> p j t", t=128), [('ps', b)],
                     [('xT', c, nt.i) for c in range(half * 4, half * 4 + 4)])

    def load_samples(nt):
        dma('sp', xin[0:NSMP, 0, :], xs_d[:, :], 'xin0', writes=[XK0])
        for half in range(2):
            b = bank()
            for j in range(4):
                c = half * 4 + j
                transpose(ps[:, b, j * NSMP:(j + 1) * NSMP], xin[0:NSMP, 0, c * 128:(c + 1) * 128], [XK0], [('ps', b)],
                          np_in=NSMP)
            copy('dve', xT[:, half * 4:(half + 1) * 4, nt.cols],
                 ps[:, b, 0:4 * NSMP].rearrange("p (j t) -> p j t", t=NSMP), [('ps', b)],
                 [('xT', c, nt.i) for c in range(half * 4, half * 4 + 4)])
        jobs = [(sca_d, 0, SA[:, :, 0, :], 'SA'), (sca_d, 1, SA[:, :, 1, :], 'SA'),
                (scb_d, 0, SB_[:, :, 0, :], 'SB_'), (scb_d, 1, SB_[:, :, 1, :], 'SB_'), (scb_d, 2, SB_[:, :, 2, :], 'SB_'),
                (slru_d, 0, SH[:, :, :], 'SH')]
        for (src, k, dst, key) in jobs:
            xi = nxt('xin', 2)
            dma('sp', xin[0:NSMP, xi, 0:512], src[:, k * 512:(k + 1) * 512], 'xin%d' % xi, writes=[('xin', xi)])
            b = bank()
            for c in range(4):
                transpose(ps[:, b, c * NSMP:(c + 1) * NSMP], xin[0:NSMP, xi, c * 128:(c + 1) * 128], [('xin', xi)],
                          [('ps', b)], np_in=NSMP)
            copy('dve', dst, ps[:, b, 0:4 * NSMP].rearrange("p (c t) -> p c t", t=NSMP), [('ps', b)], [key])

    def memory_kv():
        PHASE[0] = 'memkv'
        gi = VI["g_mem"]
        for j in range(2):
            if DEBUG.get('nomem1'):
                break
            dma('sp', xin[:, j, :], mem_d[j * 128:(j + 1) * 128, :], 'xin%d' % j, writes=[('xin', j)])
            ri = nxt('rs', NR)
            op('act', lambda e, j=j, ri=ri: e.activation(out=prod, in_=xin[:, j, :], func=AF.Square,
                                                        accum_out=rs[:, ri, 0:1]),
               reads=[('xin', j)], writes=PRODK + [('rs', ri)])
            ts('dve', rs[:, ri, 0:1], rs[:, ri, 0:1], 1.0 / D, EPS, ALU.mult, ALU.add, [('rs', ri)], [('rs', ri)])
            act(rs[:, ri, 0:1], rs[:, ri, 0:1], AF.Sqrt, [('rs', ri)], [('rs', ri)])
            op('dve', lambda e, ri=ri: e.reciprocal(out=rs[:, ri, 1:2], in_=rs[:, ri, 0:1]), reads=[('rs', ri)], writes=[('rs', ri)])
            ts('dve', xin[:, j, :], xin[:, j, :], rs[:, ri, 1:2], None, ALU.mult, None, [('xin', j), ('rs', ri)],
               [('xin', j)])
            for half in range(2):
                b = bank()
                for jj in range(4):
                    c = half * 4 + jj
                    transpose(ps[:, b, jj * 128:(jj + 1) * 128], xin[:, j, c * 128:(c + 1) * 128], [('xin', j)],
                              [('ps', b)])
                for jj in range(4):
                    c = half * 4 + jj
                    mt, mkey = memT(c)
                    ts('dve', mt[:, j * 128:(j + 1) * 128], ps[:, b, jj * 128:(jj + 1) * 128],
                       cst[:, c, gi:gi + 1], None, ALU.mult, None, [('ps', b), 'cst'], [mkey])
        if DEBUG.get('mkv') == 'A':
            slab_pos[0] = 16
            slab_issued[0] = 16
            return
        mnt = NT(0, 0, NM)

        def in_fn(kc, nt):
            return memT(kc)

        for (wn, out_d, isv) in (("xattn_wk", mk_d, False), ("xattn_wv", mv_d, True)):
            def ep(c, nt, banks, isv=isv):
                b = banks[0]
                k32, kkey = kt32(c)
                copy('act', k32, ps[:, b, :NM], [('ps', b)], [kkey])
                if not isv:
                    copy('dve', KT[:, c, :], ps[:, b, :NM], [('ps', b)], ['KT'])
            linear([[(wn, c)] for c in range(8)], 8, in_fn, [mnt], ep)
            if DEBUG.get('mkv') == 'B':
                continue
            for j in range(2):
                oi = nxt('xin', 2)
                for half in range(2):
                    b = bank()
                    for jj in range(4):
                        c = half * 4 + jj
                        k32, kkey = kt32(c)
                        transpose(ps[:, b, jj * 128:(jj + 1) * 128], k32[:, j * 128:(j + 1) * 128], [kkey], [('ps', b)])
                    copy('act', xin[:, oi, half * 512:(half + 1) * 512], ps[:, b, :], [('ps', b)], [('xin', oi)])
                    if isv:
                        copy('dve', Vn[:, j, half * 512:(half + 1) * 512], ps[:, b, :], [('ps', b)], ['Vn'])
                dma('sp', out_d[j * 128:(j + 1) * 128, :], xin[:, oi, :], 'xin%d' % oi, reads=[('xin', oi)])

    def memkv_slabs():
        return [("xattn_wk", c * 128, 8) for c in range(8)] + [("xattn_wv", c * 128, 8) for c in range(8)]

    def store_y(p, nt):
        PHASE[0] = 'p%s:store' % p
        for tcn in range(PT // 128):
            oi = nxt('xin', 2)
            col = tcn * 128
            for half in range(2):
                b = bank()
                for j in range(4):
                    c = half * 4 + j
                    transpose(ps[:, b, j * 128:(j + 1) * 128], xT[:, c, col:col + 128], [('xT', c, nt.i)], [('ps', b)])
                copy('act' if half == 0 else 'dve', xin[:, oi, half * 512:(half + 1) * 512], ps[:, b, :],
                     [('ps', b)], [('xin', oi)])
            t0 = p * PT + tcn * 128
            dma('sp', y_d[t0:t0 + 128, :], xin[:, oi, :], 'xin%d' % oi, reads=[('xin', oi)])

    def store_samples(nt):
        oi = nxt('xin', 2)
        for half in range(2):
            b = bank()
            for j in range(4):
                c = half * 4 + j
                transpose(ps[:NSMP, b, j * 128:(j + 1) * 128], xT[:, c, nt.cols], [('xT', c, nt.i)], [('ps', b)])
            copy('dve', xin[0:NSMP, oi, half * 512:(half + 1) * 512], ps[:NSMP, b, :], [('ps', b)], [('xin', oi)])
        dma('sp', ys_d[:, :], xin[0:NSMP, oi, :], 'xin%d' % oi, reads=[('xin', oi)])

    def store_sample_states():
        for base, dst in ((0, cas_d[:, 512:1024]), (4, cbs_d[:, 1024:1536]), (16, hs_d[:, :])):
            oi = nxt('xin', 2)
            b = bank()
            for c in range(4):
                transpose(ps[:NSMP, b, c * 128:(c + 1) * 128], Rs[:, base + c, :], [('Rs', base + c)], [('ps', b)])
            copy('dve', xin[0:NSMP, oi, 0:512], ps[:NSMP, b, :], [('ps', b)], [('xin', oi)])
            dma('sp', dst, xin[0:NSMP, oi, 0:512], 'xin%d' % oi, reads=[('xin', oi)])
        dma('sp', cas_d[:, 0:512], sca_d[:, 512:1024], 'ostore')
        dma('sp', cbs_d[:, 0:1024], scb_d[:, 512:1536], 'ostore')

    def store_prompt_states():
        for c in range(4):
            for (dst, src, key) in ((cap_d, carryA, 'carryA'), (cbp_d, carryB, 'carryB'), (hp_d, hcar, 'hcar')):
                op('sp', lambda e, c=c, dst=dst, src=src: e.dma_start(
                    out=dst[:, c * 128:(c + 1) * 128].rearrange("k p -> p k"), in_=src[:, c, :],
                    allow_slow_non_contiguous=True), reads=[(key, c)], dma_sem='ostore')
        dma_sems.add('ostore')

    stages = ['ffn1', 'mix', 'xattn', 'ffn2']
    nstage = len(stages) if stop is None else (stages.index(stop) + 1 if stop in stages else 0)
    per_pass = []
    if nstage >= 1:
        per_pass += ffn_slabs("ffn1_wg", "ffn1_wu", "ffn1_wd")
    if nstage >= 2:
        per_pass += mix_slabs()
    if nstage >= 3:
        per_pass += xattn_slabs()
    if nstage >= 4:
        per_pass += ffn_slabs("ffn2_wg", "ffn2_wu", "ffn2_wd")
    slabs.extend(memkv_slabs())
    for p in range(NPASS):
        slabs.extend(per_pass)

    if not DEBUG.get('nosetup'):
        setup()
    if stop != 'io0':
        memory_kv()
    else:
        slab_pos[0] = 16
        slab_issued[0] = 16
    for p in range(NPASS):
        if DEBUG.get('nopass'):
            break
        CURP[0] = p
        pnt = NT(0, 0, PT)
        nts = [pnt]
        last = (p == NPASS - 1)
        if last:
            snt = NT(1, PT, NSMP, sample=True)
            nts.append(snt)
        if p == 0:
            prefetch_x(0)
        load_x(p, pnt)
        if last:
            load_samples(snt)
        if nstage >= 1:
            ffn(nts, None, "ffn1_wg", "ffn1_wu", "ffn1_wd", "g_ffn1_pre", "g_ffn1_post")
        if nstage >= 2:
            mix(nts, p)
        if nstage >= 3:
            xattn(nts, p)
        if not last:
            prefetch_x(p + 1)
        if nstage >= 4:
            ffn(nts, None, "ffn2_wg", "ffn2_wu", "ffn2_wd", "g_ffn2_pre", "g_ffn2_post")
        store_y(p, pnt)
        if last:
            store_samples(snt)
            if nstage >= 2:
                store_sample_states()
    if nstage >= 2:
        store_prompt_states()
    assert slab_pos[0] == len(slabs), (slab_pos[0], len(slabs))

    semh = {}
    for name in list(S.ENG) + sorted(dma_sems):
        semh[name] = es.enter_context(nc.semaphore(name))
    hw = {'pe': 'tensor', 'act': 'scalar', 'dve': 'vector', 'pool': 'gpsimd', 'sp': 'sync'}

    def replay(name, e):
        for waits, fn, inc, ph in S.streams[name]:
            for s, v in waits:
                e.wait_ge(semh[s], v)
            ins = fn(e)
            if DEBUG.get('annot'):
                ins.annotate(ph)
            ins.then_inc(semh[inc[0]], inc[1])
        if name == 'sp':
            for s in sorted(dma_sems):
                if S.count.get(s, 0) > 0:
                    e.wait_ge(semh[s], S.count[s])
            for s in ('pe', 'act', 'dve', 'pool'):
                if S.count.get(s, 0) > 0:
                    e.wait_ge(semh[s], S.count[s])

    with nc.Block() as block:
        @block.tensor
        def _(e):
            replay('pe', e)

        @block.scalar
        def _(e):
            replay('act', e)

        @block.vector
        def _(e):
            replay('dve', e)

        @block.gpsimd
        def _(e):
            replay('pool', e)

        @block.sync
        def _(e):
            replay('sp', e)
    es.close()
    return nc


_W_NAMES = ["ffn1_wg", "ffn1_wu", "ffn1_wd", "w_in", "w_out", "xattn_wq", "xattn_wk", "xattn_wv", "xattn_wo",
            "ffn2_wg", "ffn2_wu", "ffn2_wd"]
_V_NAMES = ["g_ffn1_pre", "g_ffn1_post", "g_mix_pre", "g_mix_post", "g_xattn_pre", "g_xattn_post", "g_mem",
            "g_ffn2_pre", "g_ffn2_post", "conv_b_b", "lru_ba", "lru_bx", "lru_lam"]


def make_in_maps(inputs):
    f = lambda a: np.ascontiguousarray(np.asarray(a, dtype=np.float32))
    shared = {}
    for n in _W_NAMES:
        shared[n] = f(inputs[n][0])
    for n in _V_NAMES:
        shared[n] = f(inputs[n][0]).reshape(1, -1)
    shared["conv_a_w"] = f(inputs["conv_a_w"][0])
    shared["conv_b_w"] = f(inputs["conv_b_w"][0])
    shared["lru_wa"] = f(inputs["lru_wa"][0])
    shared["lru_wx"] = f(inputs["lru_wx"][0])
    maps = []
    for b in range(8):
        m = dict(shared)
        sl = slice(b * NSMP, (b + 1) * NSMP)
        m["x"] = f(inputs["x_prompt"][b])
        m["xs"] = f(inputs["x_sample"][sl, 0, :])
        m["mem"] = f(inputs["mem_prompt"][b])
        m["ck"] = f(inputs["cache_mem_k"][0, sl]).reshape(NSMP, NM, D)
        m["cv"] = f(inputs["cache_mem_v"][0, sl]).reshape(NSMP, NM, D)
        m["sca"] = f(inputs["state_conv_a"][0, sl]).reshape(NSMP, 1024)
        m["scb"] = f(inputs["state_conv_b"][0, sl]).reshape(NSMP, 1536)
        m["slru"] = f(inputs["state_lru"][0, sl]).reshape(NSMP, 512)
        maps.append(m)
    return maps


def assemble(results):
    cat = lambda k: np.stack([r[k] for r in results], axis=0)
    yp = cat("y")
    ys = np.concatenate([r["ys"] for r in results], axis=0).reshape(128, 1, D)
    mk = cat("mk").reshape(1, 8, NM, 4, 256)
    mv = cat("mv").reshape(1, 8, NM, 4, 256)
    cap = cat("cap").reshape(1, 8, 2, 512)
    cbp = cat("cbp").reshape(1, 8, 3, 512)
    hp = cat("hp").reshape(1, 8, 512)
    cas = np.concatenate([r["cas"] for r in results], axis=0).reshape(1, 128, 2, 512)
    cbs = np.concatenate([r["cbs"] for r in results], axis=0).reshape(1, 128, 3, 512)
    hs = np.concatenate([r["hs"] for r in results], axis=0).reshape(1, 128, 512)
    return tuple(np.ascontiguousarray(a, dtype=np.float32) for a in (yp, ys, mk, mv, cap, cbp, hp, cas, cbs, hs))


def kernel(**inputs):
    nc = build()
    maps = make_in_maps(inputs)
    res = run_bass_kernel_spmd(nc, maps, core_ids=list(range(8)))
    return assemble(res.results)
```
